# Optimizing a Trainium2 kernel written in Bass

```python
import math
import jax, jax.numpy as jnp
from jax import lax
import numpy as np

D_MODEL = 1024
BATCH = 32
SEQ = 256
DEPTH = 1
DEC_BATCH = 4
DEC_SEQ = 2048
PAST_LEN = 512

GRID_W = 64
HEAD_DIM = 64
A_HEADS = 8
A_QK = A_HEADS * 2 * HEAD_DIM
A_V = A_HEADS * 2 * HEAD_DIM
B_HEADS = 16
B_KV_HEADS = 4
B_GROUP = B_HEADS // B_KV_HEADS
B_Q = B_HEADS * HEAD_DIM
B_KVW = B_KV_HEADS * HEAD_DIM
WINDOW = 128
BLOCK = 128
N_IN = 2 * A_QK + A_V + B_Q + 2 * B_KVW + 2 * D_MODEL
D_FF = 2816
N_MOD = 6
EPS = 1e-6
ROPE_BASE = 10000.0
NEG = -1e30

kernel_name = "hybrid_diffattn_window_sink_dit_step"


def _rmsnorm(x, g):
    xf = x.astype(jnp.float32)
    y = xf * lax.rsqrt(jnp.mean(xf * xf, axis=-1, keepdims=True) + EPS)
    return (y * g.astype(jnp.float32)).astype(x.dtype)


def _axial_rope(x):
    T = x.shape[1]
    rows = T // GRID_W
    row = jnp.repeat(jnp.arange(rows), GRID_W).astype(jnp.float32)
    col = jnp.tile(jnp.arange(GRID_W), rows).astype(jnp.float32)
    quarter = HEAD_DIM // 4
    freqs = ROPE_BASE ** (-jnp.arange(quarter, dtype=jnp.float32) / quarter)
    bshape = (1, T) + (1,) * (x.ndim - 3) + (quarter,)
    ang_r = (row[:, None] * freqs).reshape(bshape)
    ang_c = (col[:, None] * freqs).reshape(bshape)

    def rot(xh, ang):
        x1, x2 = xh[..., :quarter], xh[..., quarter:]
        cos, sin = jnp.cos(ang), jnp.sin(ang)
        return jnp.concatenate([x1 * cos - x2 * sin, x2 * cos + x1 * sin], axis=-1)

    xf = x.astype(jnp.float32)
    half = HEAD_DIM // 2
    out = jnp.concatenate([rot(xf[..., :half], ang_r), rot(xf[..., half:], ang_c)], axis=-1)
    return out.astype(x.dtype)


def _diff_attend(q, k, v, lam):
    B, T = q.shape[:2]
    nb = T // BLOCK
    qb = q.reshape(B, nb, BLOCK, A_HEADS, 2, HEAD_DIM).swapaxes(0, 1)
    scale = HEAD_DIM ** -0.5

    def one(qi):
        s = jnp.einsum("bqhcd,bkhcd->bhcqk", qi, k).astype(jnp.float32) * scale
        p = jax.nn.softmax(s, axis=-1)
        a = (p[:, :, 0] - lam * p[:, :, 1]).astype(v.dtype)
        return jnp.einsum("bhqk,bkhe->bqhe", a, v)

    out = lax.map(one, qb)
    return out.swapaxes(0, 1).reshape(B, T, A_HEADS, 2 * HEAD_DIM)


def _window_attend(q, k_ctx, v_ctx, sink, k_lat, v_lat):
    B, T = q.shape[:2]
    nb = T // BLOCK
    n_ctx = k_ctx.shape[1]
    scale = HEAD_DIM ** -0.5
    qb = q.reshape(B, nb, BLOCK, B_KV_HEADS, B_GROUP, HEAD_DIM).swapaxes(0, 1)
    sink_col = jnp.broadcast_to(sink.astype(jnp.float32).reshape(1, B_KV_HEADS, B_GROUP, 1, 1),
                                (B, B_KV_HEADS, B_GROUP, BLOCK, 1))
    local = k_lat is not None
    if local:
        pad = ((0, 0), (BLOCK, BLOCK), (0, 0), (0, 0))
        kpad = jnp.pad(k_lat, pad)
        vpad = jnp.pad(v_lat, pad)
        rel = jnp.arange(3 * BLOCK)[None, :] - BLOCK - jnp.arange(BLOCK)[:, None]
        band = jnp.abs(rel) <= WINDOW

    def one(args):
        i, qi = args
        s_ctx = jnp.einsum("bqgrd,bkgd->bgrqk", qi, k_ctx).astype(jnp.float32) * scale
        if local:
            kb = lax.dynamic_slice_in_dim(kpad, i * BLOCK, 3 * BLOCK, axis=1)
            vb = lax.dynamic_slice_in_dim(vpad, i * BLOCK, 3 * BLOCK, axis=1)
            s_loc = jnp.einsum("bqgrd,bkgd->bgrqk", qi, kb).astype(jnp.float32) * scale
            kpos = i * BLOCK - BLOCK + jnp.arange(3 * BLOCK)
            valid = band & ((kpos >= 0) & (kpos < T))[None, :]
            s_loc = jnp.where(valid, s_loc, NEG)
            p = jax.nn.softmax(jnp.concatenate([s_loc, s_ctx, sink_col], axis=-1), axis=-1)
            p = p.astype(v_ctx.dtype)
            n_loc = 3 * BLOCK
            out = (jnp.einsum("bgrqk,bkgd->bqgrd", p[..., :n_loc], vb)
                   + jnp.einsum("bgrqk,bkgd->bqgrd", p[..., n_loc:n_loc + n_ctx], v_ctx))
        else:
            p = jax.nn.softmax(jnp.concatenate([s_ctx, sink_col], axis=-1), axis=-1)
            p = p.astype(v_ctx.dtype)
            out = jnp.einsum("bgrqk,bkgd->bqgrd", p[..., :n_ctx], v_ctx)
        return out

    out = lax.map(one, (jnp.arange(nb), qb))
    return out.swapaxes(0, 1).reshape(B, T, B_HEADS, HEAD_DIM)


def _layer(x, mod, ctx_kv, lambda_init, norm1_g, w_in, qn_a, kn_a, lq1, lk1, lq2, lk2, subln_g,
           qn_b, kn_b, sink, w_oa, w_ob, w_out, norm2_g, w_gate, w_up, w_down):
    B, T = x.shape[:2]
    sh1, sc1, g1, sh2, sc2, g2 = jnp.split(mod, N_MOD, axis=-1)
    h = _rmsnorm(x, norm1_g) * (1 + sc1) + sh1
    proj = h @ w_in
    offs = list(np.cumsum([A_QK, A_QK, A_V, B_Q, B_KVW, B_KVW, D_MODEL]))
    qa, ka, va, qb, kb, vb, ga, gb = jnp.split(proj, offs, axis=-1)
    qa = _rmsnorm(qa.reshape(B, T, A_HEADS, 2, HEAD_DIM), qn_a)
    ka = _rmsnorm(ka.reshape(B, T, A_HEADS, 2, HEAD_DIM), kn_a)
    va = va.reshape(B, T, A_HEADS, 2 * HEAD_DIM)
    qb = _rmsnorm(qb.reshape(B, T, B_HEADS, HEAD_DIM), qn_b)
    kb = _rmsnorm(kb.reshape(B, T, B_KV_HEADS, HEAD_DIM), kn_b)
    vb = vb.reshape(B, T, B_KV_HEADS, HEAD_DIM)
    f32 = lambda a: a.astype(jnp.float32)
    lam = (jnp.exp(jnp.sum(f32(lq1) * f32(lk1))) - jnp.exp(jnp.sum(f32(lq2) * f32(lk2)))
           + lambda_init)
    if ctx_kv is None:
        oa = _diff_attend(qa, ka, va, lam)
        ob = _window_attend(qb, kb, vb, sink, None, None)
        new_kv = (ka, va, kb, vb)
    else:
        cka, cva, ckb, cvb = ctx_kv
        qa, ka, qb, kb = _axial_rope(qa), _axial_rope(ka), _axial_rope(qb), _axial_rope(kb)
        oa = _diff_attend(qa, jnp.concatenate([ka, cka], axis=1),
                          jnp.concatenate([va, cva], axis=1), lam)
        ob = _window_attend(qb, ckb, cvb, sink, kb, vb)
        new_kv = None
    oa = _rmsnorm(oa, subln_g) * (1.0 - lambda_init)
    oa = oa.reshape(B, T, A_V) @ w_oa
    ob = ob.reshape(B, T, B_Q) @ w_ob
    merged = jax.nn.sigmoid(ga) * oa + jax.nn.sigmoid(gb) * ob
    x = x + g1 * (merged @ w_out)
    h2 = _rmsnorm(x, norm2_g) * (1 + sc2) + sh2
    x = x + g2 * ((jax.nn.silu(h2 @ w_gate) * (h2 @ w_up)) @ w_down)
    return x, new_kv


def setup_inputs(seed: int = 0) -> dict:
    key = jax.random.key(seed)
    ks = jax.random.split(key, 32)
    n = lambda k, s: jax.random.normal(k, s, jnp.float32)
    gain = lambda k, s: 1.0 + 0.02 * n(k, s)
    L = DEPTH
    return {
        "x_prompt": n(ks[0], (BATCH, SEQ, D_MODEL)),
        "x_sample": n(ks[1], (DEC_BATCH, DEC_SEQ, D_MODEL)),
        "cache_diff_k": n(ks[2], (DEC_BATCH, L, PAST_LEN, A_HEADS, 2, HEAD_DIM)),
        "cache_diff_v": n(ks[3], (DEC_BATCH, L, PAST_LEN, A_HEADS, 2 * HEAD_DIM)),
        "cache_win_k": n(ks[4], (DEC_BATCH, L, PAST_LEN, B_KV_HEADS, HEAD_DIM)),
        "cache_win_v": n(ks[5], (DEC_BATCH, L, PAST_LEN, B_KV_HEADS, HEAD_DIM)),
        "c": n(ks[6], (DEC_BATCH, D_MODEL)),
        "c_ctx": n(ks[7], (D_MODEL,)),
        "w_ada": 0.3 * D_MODEL ** -0.5 * n(ks[8], (L, D_MODEL, N_MOD * D_MODEL)),
        "b_ada": 0.02 * n(ks[9], (L, N_MOD * D_MODEL)),
        "norm1_g": gain(ks[10], (L, D_MODEL)),
        "w_in": D_MODEL ** -0.5 * n(ks[11], (L, D_MODEL, N_IN)),
        "qn_a": gain(ks[12], (L, HEAD_DIM)),
        "kn_a": gain(ks[13], (L, HEAD_DIM)),
        "lambda_q1": 0.1 * n(ks[14], (L, HEAD_DIM)),
        "lambda_k1": 0.1 * n(ks[15], (L, HEAD_DIM)),
        "lambda_q2": 0.1 * n(ks[16], (L, HEAD_DIM)),
        "lambda_k2": 0.1 * n(ks[17], (L, HEAD_DIM)),
        "subln_g": gain(ks[18], (L, 2 * HEAD_DIM)),
        "qn_b": gain(ks[19], (L, HEAD_DIM)),
        "kn_b": gain(ks[20], (L, HEAD_DIM)),
        "sink": 0.5 * n(ks[21], (L, B_HEADS)),
        "w_oa": A_V ** -0.5 * n(ks[22], (L, A_V, D_MODEL)),
        "w_ob": B_Q ** -0.5 * n(ks[23], (L, B_Q, D_MODEL)),
        "w_out": D_MODEL ** -0.5 * n(ks[24], (L, D_MODEL, D_MODEL)),
        "norm2_g": gain(ks[25], (L, D_MODEL)),
        "w_gate": D_MODEL ** -0.5 * n(ks[26], (L, D_MODEL, D_FF)),
        "w_up": D_MODEL ** -0.5 * n(ks[27], (L, D_MODEL, D_FF)),
        "w_down": D_FF ** -0.5 * n(ks[28], (L, D_FF, D_MODEL)),
    }


def reference(x_prompt, x_sample, cache_diff_k, cache_diff_v, cache_win_k, cache_win_v, c, c_ctx,
              w_ada, b_ada, norm1_g, w_in, qn_a, kn_a, lambda_q1, lambda_k1, lambda_q2, lambda_k2,
              subln_g, qn_b, kn_b, sink, w_oa, w_ob, w_out, norm2_g, w_gate, w_up, w_down):
    xp, xs = x_prompt, x_sample
    dk, dv, wk, wv = [], [], [], []
    for l in range(DEPTH):
        lambda_init = 0.8 - 0.6 * math.exp(-0.3 * l)
        mod_ctx = (jax.nn.silu(c_ctx) @ w_ada[l] + b_ada[l])[None, None, :]
        mod_lat = (jax.nn.silu(c) @ w_ada[l] + b_ada[l])[:, None, :]
        wts = (lambda_init, norm1_g[l], w_in[l], qn_a[l], kn_a[l], lambda_q1[l], lambda_k1[l],
               lambda_q2[l], lambda_k2[l], subln_g[l], qn_b[l], kn_b[l], sink[l], w_oa[l], w_ob[l],
               w_out[l], norm2_g[l], w_gate[l], w_up[l], w_down[l])
        xp, kv = _layer(xp, mod_ctx, None, *wts)
        dk.append(kv[0]); dv.append(kv[1]); wk.append(kv[2]); wv.append(kv[3])
        ctx = (cache_diff_k[:, l], cache_diff_v[:, l], cache_win_k[:, l], cache_win_v[:, l])
        xs, _ = _layer(xs, mod_lat, ctx, *wts)
    new_diff_k = jnp.stack(dk, axis=1)
    new_diff_v = jnp.stack(dv, axis=1)
    new_win_k = jnp.stack(wk, axis=1)
    new_win_v = jnp.stack(wv, axis=1)
    return (xp, xs, new_diff_k, new_diff_v, new_win_k, new_win_v)
```

```python
import contextlib
import numpy as np
import concourse.bass as bass
import concourse.mybir as mybir
from concourse.bass_utils import run_bass_kernel_spmd

F32 = mybir.dt.float32
BF16 = mybir.dt.bfloat16
AF = mybir.ActivationFunctionType
ALU = mybir.AluOpType
AX = mybir.AxisListType

_DT_SIZE = {mybir.dt.float32: 4, mybir.dt.bfloat16: 2}
N_DMA_SLOTS = 20
EPS = 1e-6


class _Op:
    __slots__ = ("eng", "fn", "deps", "needs_inc", "val", "dma", "slot", "dval", "attach")

    def __init__(self, eng, fn, dma=False):
        self.eng = eng
        self.fn = fn
        self.attach = (eng != "pe")
        self.deps = []
        self.needs_inc = False
        self.val = None
        self.dma = dma
        self.slot = None
        self.dval = None


def _region(ap):
    t = ap.tensor
    space = str(ap.space)
    if space == "PSUM":
        return (t.name, 0, 128, 0, 1 << 30, True, True)
    esz = _DT_SIZE[ap.dtype]
    dims = list(ap.ap)
    if space == "SB":
        pstep, pcount = dims[0]
        if pstep == 0:
            pstep = 1 << 40
        p0 = ap.offset // pstep
        f0 = ap.offset % pstep
        rest = dims[1:]
    else:
        p0, pcount, f0, rest = 0, 1, ap.offset, dims
    ext = 0
    n = 1
    for s, c in rest:
        ext += (c - 1) * abs(s)
        if s != 0:
            n *= c
    lo = f0 * esz
    hi = (f0 + ext + 1) * esz
    return (t.name, p0, p0 + pcount, lo, hi, n == ext + 1, False)


class Builder:
    ENG = ("pe", "act", "dve", "pool", "sp")

    def __init__(self, nc, ro_dram=()):
        self.nc = nc
        self.streams = {e: [] for e in self.ENG}
        self.recs = {}
        self.dma_count = {e: 0 for e in self.ENG}
        self.ro_dram = set(ro_dram)
        self.nops = 0

    def _track(self, op, reads, writes):
        deps = []
        acc = []
        for ap in reads:
            if str(ap.space) == "DRAM" and ap.tensor.name in self.ro_dram:
                continue
            acc.append((_region(ap), "r"))
        for ap in writes:
            acc.append((_region(ap), "w"))
        for (name, p0, p1, lo, hi, dense, psum), k in acc:
            lst = self.recs.setdefault(name, [])
            keep = []
            for r in lst:
                if r[0] < p1 and p0 < r[1] and r[2] < hi and lo < r[3]:
                    d = r[4]
                    if d is not op and (r[5] == "w" or k == "w" or psum):
                        if d.eng == op.eng and not d.dma and not op.dma:
                            if op.eng != "pe" and r[5] == "w":
                                deps.append(d)
                        else:
                            deps.append(d)
                    if (k == "w" or psum) and dense and p0 <= r[0] and r[1] <= p1 and lo <= r[2] and r[3] <= hi:
                        continue
                keep.append(r)
            lst[:] = keep
        for (name, p0, p1, lo, hi, dense, psum), k in acc:
            lst = self.recs[name]
            if k == "r" and not op.dma:
                for r in lst:
                    if r[5] == "r" and r[0] == p0 and r[1] == p1 and r[2] == lo and r[3] == hi \
                            and r[4].eng == op.eng and not r[4].dma:
                        r[4] = op
                        break
                else:
                    lst.append([p0, p1, lo, hi, op, k])
            else:
                lst.append([p0, p1, lo, hi, op, k])
        seen = set()
        for d in deps:
            if id(d) in seen:
                continue
            seen.add(id(d))
            d.needs_inc = True
            op.deps.append(d)

    def op(self, eng, fn, reads=(), writes=(), attach=None):
        o = _Op(eng, fn)
        if attach is not None:
            o.attach = attach and eng != "pe"
        self._track(o, list(reads), list(writes))
        self.streams[eng].append(o)
        self.nops += 1
        return o

    def dma(self, q, out, in_):
        o = _Op(q, lambda e: e.dma_start(out=out, in_=in_), dma=True)
        j = self.dma_count[q]
        self.dma_count[q] = j + 1
        o.slot = j % N_DMA_SLOTS
        o.dval = 16 * (j // N_DMA_SLOTS + 1)
        self._track(o, [in_], [out])
        self.streams[q].append(o)
        self.nops += 1
        return o

    def emit(self):
        nc = self.nc
        with contextlib.ExitStack() as es:
            esem = {e: es.enter_context(nc.semaphore(f"s_{e}")) for e in self.ENG}
            dsem = {e: [es.enter_context(nc.semaphore(f"d_{e}{i}")) for i in range(N_DMA_SLOTS)]
                    for e in self.ENG if self.dma_count[e] > 0}
            for e in self.ENG:
                c = 0
                for o in self.streams[e]:
                    if not o.dma and o.needs_inc:
                        c += 1
                        o.val = c
            block = es.enter_context(nc.Block())
            regs = {"pe": block.tensor, "act": block.scalar, "dve": block.vector,
                    "pool": block.gpsimd, "sp": block.sync}

            def make(e):
                stream = self.streams[e]

                def body(eng):
                    waited = {}

                    def wait(sem, val):
                        k = id(sem)
                        if waited.get(k, 0) < val:
                            eng.wait_ge(sem, val)
                            waited[k] = val
                    for o in stream:
                        need = {}
                        for d in o.deps:
                            sem, val = (dsem[d.eng][d.slot], d.dval) if d.dma else (esem[d.eng], d.val)
                            if need.get(id(sem), (None, 0))[1] < val:
                                need[id(sem)] = (sem, val)
                        if o.dma and o.dval > 16:
                            sem, val = dsem[e][o.slot], o.dval - 16
                            if need.get(id(sem), (None, 0))[1] < val:
                                need[id(sem)] = (sem, val)
                        waits = [(sem, val) for k, (sem, val) in need.items() if waited.get(k, 0) < val]
                        for sem, val in waits:
                            waited[id(sem)] = val
                        att = waits.pop() if (waits and o.attach) else None
                        for sem, val in waits:
                            eng.wait_ge(sem, val)
                        ins = o.fn(eng)
                        if att is not None:
                            ins.wait_op(att[0], att[1], "sem-ge")
                        if o.dma:
                            ins.then_inc(dsem[e][o.slot], 16)
                        elif o.needs_inc:
                            ins.then_inc(esem[e], 1)
                    n = self.dma_count[e]
                    if n > 0:
                        for s in range(N_DMA_SLOTS):
                            cnt = (n - s + N_DMA_SLOTS - 1) // N_DMA_SLOTS
                            if cnt > 0:
                                wait(dsem[e][s], 16 * cnt)
                return body
            for e in self.ENG:
                if self.streams[e]:
                    regs[e](make(e))


D = 1024
NIN = 6656
DFF = 2816
NFF = 22
LAMBDA_INIT = 0.8 - 0.6 * 1.0
KEEP_WARM = False


def isap(x):
    return hasattr(x, "tensor")


def build_program():
    nc = bass.Bass("TRN2", target_bir_lowering=False)
    ro = []

    def din(name, shape):
        ro.append(name)
        return nc.dram_tensor(name, shape, F32, kind="ExternalInput").ap()

    def dout(name, shape):
        return nc.dram_tensor(name, shape, F32, kind="ExternalOutput").ap()

    xp = din("xp", [1024, D])
    xs = din("xs", [2048, D])
    cdk = din("cdk", [512, 1024])
    cdv = din("cdv", [512, 1024])
    cwk = din("cwk", [512, 256])
    cwv = din("cwv", [512, 256])
    cvec = din("cvec", [128, 16])
    w_ada = din("w_ada", [D, 6 * D])
    b_adaT = din("b_adaT", [128, 48])
    ngT = din("ngT", [128, 16])
    w_in = din("w_in", [D, NIN])
    gains = din("gains", [4 * 64])
    lams = din("lams", [4 * 64])
    subln = din("subln", [128])
    sink = din("sink", [16])
    sinkC = din("sinkC", [128, 4])
    w_oa = din("w_oa", [D, D])
    w_ob = din("w_ob", [D, D])
    w_out = din("w_out", [D, D])
    w_gate = din("w_gate", [D, DFF])
    w_up = din("w_up", [D, DFF])
    w_down = din("w_down", [DFF, D])
    ident = din("ident", [128, 128])
    ropec = din("ropec", [128, 16 * 64])
    ropes = din("ropes", [128, 16 * 64])
    masks = din("masks", [128, 4 * 128])

    yp = dout("yp", [1024, D])
    ys = dout("ys", [1024, D])
    ndk = dout("ndk", [1024, 1024])
    ndv = dout("ndv", [1024, 1024])
    nwk = dout("nwk", [1024, 256])
    nwv = dout("nwv", [1024, 256])

    es = contextlib.ExitStack()
    with es:
        def sb(name, shape, dt):
            return es.enter_context(nc.sbuf_tensor(name, shape, dt))

        B = Builder(nc, ro_dram=ro)

        ARENA = 105 * 1024
        arena = sb("arena", [128, ARENA // 2], BF16)
        SCR = 40 * 1024
        scr = sb("scr", [128, SCR // 2], BF16)

        def view(t, off, shape, dt):
            n = int(np.prod(shape))
            if dt == BF16:
                v = t[:, off // 2: off // 2 + n]
            else:
                v = t[:, off // 2: off // 2 + 2 * n].bitcast(F32)
            if len(shape) == 1:
                return v
            names = " ".join(f"d{i}" for i in range(len(shape)))
            kw = {f"d{i}": s for i, s in enumerate(shape)}
            return v.rearrange(f"p ({names}) -> p {names}", **kw)

        def AR(off, shape, dt=BF16):
            return view(arena, off, shape, dt)

        def SC(off, shape, dt=F32):
            return view(scr, off, shape, dt)

        wslot = [sb(f"wslot{i}", [128, 4096], BF16) for i in range(4)]
        PS = [es.enter_context(nc.psum_tensor(f"ps{i}", [128, 512], F32)) for i in range(8)]

        def psb(i, half):
            return PS[i][:, half * 256:(half + 1) * 256].bitcast(BF16)

        ident_f = sb("ident_f", [128, 128], F32)
        ident_b = sb("ident_b", [128, 128], BF16)
        negh = sb("negh", [128, 64], F32)
        g4 = sb("g4", [128, 4, 64], F32)
        lam4 = sb("lam4", [128, 4, 64], F32)
        sublnG = sb("sublnG", [128, 128], F32)
        esink = sb("esink", [128, 16], F32)
        smallc = sb("smallc", [128, 16], F32)
        cosT = sb("cosT", [128, 16, 64], F32)
        sinT = sb("sinT", [128, 16, 2, 2, 16], F32)
        maskb = sb("maskb", [128, 4, 128], BF16)
        cv = sb("cv", [128, 8, 2], F32)
        scT = sb("scT", [128, 8, 2], BF16)
        badaT = sb("badaT", [128, 48], F32)
        ngt = sb("ngt", [128, 16], F32)
        modT = sb("modT", [128, 48, 2], F32)
        AB = sb("AB", [128, 2, 4, 8], F32)
        gbc = sb("gbc", [128, 2, D], F32)
        stats = sb("stats", [128, 512], F32)
        ones_b = sb("ones_b", [128, 128], BF16)
        GB5 = sb("GB5", [128, 5, 64], F32)
        Sel = sb("Sel", [128, 4, 128], F32)
        esC = sb("esC", [128, 4], F32)
        sublnC = sb("sublnC", [128, 1], F32)
        s_ctr = [0]
        b_ctr = [0]
        a_ctr = [0]

        def MM(out, lhsT, rhs, start=True, stop=True, skip=False):
            B.op("pe", lambda e: e.matmul(out, lhsT, rhs, start=start, stop=stop, skip_group_check=skip),
                 reads=[lhsT, rhs], writes=[out])

        def MMT(out, lhsT, rhs, start, stop, tpos):
            B.op("pe", lambda e: e.matmul(out, lhsT, rhs, start=start, stop=stop, skip_group_check=True, tile_position=tpos),
                 reads=[lhsT, rhs], writes=[out])

        def TR(out, in_, idn):
            B.op("pe", lambda e: e.transpose(out, in_, idn), reads=[in_, idn], writes=[out])

        def ACT(out, in_, func, scale=None, bias=None, accum=None):
            kw = {}
            reads = [in_]
            writes = [out]
            if scale is not None:
                kw["scale"] = scale
                if isap(scale):
                    reads.append(scale)
            if bias is not None:
                kw["bias"] = bias
                if isap(bias):
                    reads.append(bias)
            if accum is not None:
                kw["accum_out"] = accum
                writes.append(accum)
            B.op("act", lambda e: e.activation(out, in_, func, **kw), reads=reads, writes=writes, attach=(accum is None))

        def TT(eng, out, a, b, op):
            B.op(eng, lambda e: e.tensor_tensor(out, a, b, op=op), reads=[a, b], writes=[out])

        def TS(eng, out, a, s1, s2, op0, op1=None):
            reads = [a] + [s for s in (s1, s2) if isap(s)]
            if op1 is None:
                B.op(eng, lambda e: e.tensor_scalar(out, a, s1, None, op0=op0), reads=reads, writes=[out])
            else:
                B.op(eng, lambda e: e.tensor_scalar(out, a, s1, s2, op0=op0, op1=op1), reads=reads, writes=[out])

        def STT(out, a, s, b, op0, op1):
            reads = [a, b] + ([s] if isap(s) else [])
            B.op("dve", lambda e: e.scalar_tensor_tensor(out, a, s, b, op0=op0, op1=op1), reads=reads, writes=[out])

        def CP(eng, out, in_):
            if eng == "act":
                ACT(out, in_, AF.Copy)
            else:
                B.op(eng, lambda e: e.tensor_copy(out, in_), reads=[in_], writes=[out])

        def RED(out, in_):
            B.op("dve", lambda e: e.tensor_reduce(out, in_, axis=AX.X, op=ALU.add), reads=[in_], writes=[out])

        def RECIP(out, in_):
            B.op("dve", lambda e: e.reciprocal(out, in_), reads=[in_], writes=[out])

        def MEMSET(eng, ap, val):
            B.op(eng, lambda e: e.memset(ap, val), writes=[ap])

        def RSTD(out, ss, n):
            TS("pool", out, ss, 1.0 / n, EPS, ALU.mult, ALU.add)
            w = out.shape[-1] if len(out.shape) > 1 else 1
            TT("pool", out, out, negh[:, 0:w], ALU.pow)

        stat_ctr = [0]

        def stat(n):
            c = stat_ctr[0]
            if c + n > 512:
                c = 0
            stat_ctr[0] = c + n
            return stats[:, c:c + n]

        bank_ctr = [0]

        def nbank(lo=0, hi=6):
            i = bank_ctr[0]
            bank_ctr[0] = i + 1
            return lo + i % (hi - lo)

        tp_ctr = [0]

        def tpbank(banks=(6, 7)):
            i = tp_ctr[0]
            tp_ctr[0] = i + 1
            return banks[i % len(banks)]

        class WStream:
            def __init__(self):
                self.plan = []
                self.issued = 0
                self.used = 0

            def _issue(self):
                while self.issued < len(self.plan) and self.issued < self.used + 3:
                    s = wslot[self.issued % 4]
                    for src, dst in self.plan[self.issued]:
                        B.dma("pool", dst(s), src)
                    self.issued += 1

            def load(self, pieces=None):
                self._issue()
                s = wslot[self.used % 4]
                self.used += 1
                assert self.used <= self.issued
                return s
        WS = WStream()

        def wv(s, k, n):
            return s[:, 0:k * n].rearrange("p (k n) -> p k n", k=k)

        w_in_v = w_in.rearrange("(k p) n -> p k n", p=128)
        w_ada_v = w_ada.rearrange("(k p) n -> p k n", p=128)

        def cols(wview, c0, n, k=8):
            return [(wview[:, :, c0:c0 + n], lambda s: wv(s, k, n))]

        def load_cols(wview, c0, n, k=8):
            return WS.load()

        w_oa_v = w_oa.rearrange("(k p) n -> p k n", p=128)
        w_ob_v = w_ob.rearrange("(k p) n -> p k n", p=128)
        w_out_v = w_out.rearrange("(k p) n -> p k n", p=128)
        w_gate_v = w_gate.rearrange("(k p) n -> p k n", p=128)
        w_up_v = w_up.rearrange("(k p) n -> p k n", p=128)
        w_down_v = w_down.rearrange("(f p) n -> p f n", p=128)

        def sub(k, n, a, b):
            return lambda s: wv(s, k, n)[:, :, a:b]

        def plan_pass():
            pl = []
            for h in range(8):
                pl.append([(w_in_v[:, :, h * 128:(h + 1) * 128], sub(8, 384, 0, 128)),
                           (w_in_v[:, :, 1024 + h * 128:1024 + (h + 1) * 128], sub(8, 384, 128, 256)),
                           (w_in_v[:, :, 2048 + h * 128:2048 + (h + 1) * 128], sub(8, 384, 256, 384))])
            for g in range(4):
                pl.append([(w_in_v[:, :, 3072 + g * 256:3072 + (g + 1) * 256], sub(8, 384, 0, 256)),
                           (w_in_v[:, :, 4096 + g * 64:4096 + (g + 1) * 64], sub(8, 384, 256, 320)),
                           (w_in_v[:, :, 4352 + g * 64:4352 + (g + 1) * 64], sub(8, 384, 320, 384))])
            for n in range(8):
                pl.append([(w_oa_v[:, :, n * 128:(n + 1) * 128], sub(8, 512, 0, 128)),
                           (w_ob_v[:, :, n * 128:(n + 1) * 128], sub(8, 512, 128, 256)),
                           (w_in_v[:, :, 4608 + n * 128:4608 + (n + 1) * 128], sub(8, 512, 256, 384)),
                           (w_in_v[:, :, 5632 + n * 128:5632 + (n + 1) * 128], sub(8, 512, 384, 512))])
            for nh in range(2):
                pl.append(cols(w_out_v, nh * 512, 512))
            for fg in range(6):
                nc_ = 512 if fg < 5 else 256
                pl.append(cols(w_gate_v, fg * 512, nc_))
                pl.append(cols(w_up_v, fg * 512, nc_))
            for nh in range(2):
                for fg in range(6):
                    nf = 4 if fg < 5 else 2
                    pl.append([(w_down_v[:, fg * 4:fg * 4 + nf, nh * 512:(nh + 1) * 512], lambda s, nf=nf: wv(s, nf, 512))])
            return pl
        WS.plan = [cols(w_ada_v, j * 512, 512) for j in range(12)] + plan_pass() + plan_pass()

        B.dma("sp", ident_f[:], ident)
        B.dma("pool", ident_b[:], ident)
        MEMSET("pool", negh[:], -0.5)
        MEMSET("dve", stats[:], 0.0)
        MEMSET("pool", ones_b[:], 1.0)
        B.dma("sp", sublnC[:], subln.rearrange("(p o) -> p o", o=1))
        TS("dve", sublnC[:], sublnC[:], 1.0 - LAMBDA_INIT, None, ALU.mult)
        B.dma("sp", cv[:].rearrange("p k t -> p (k t)"), cvec)
        B.dma("sp", badaT[:], b_adaT)
        B.dma("sp", ngt[:], ngT)
        B.dma("sp", g4[:].rearrange("p a d -> p (a d)"), gains.partition_broadcast(128))
        B.dma("sp", lam4[:].rearrange("p a d -> p (a d)"), lams.partition_broadcast(128))
        B.dma("sp", sublnG[:], subln.partition_broadcast(128))
        B.dma("sp", esink[:], sink.partition_broadcast(128))
        B.dma("sp", cosT[:].rearrange("p a d -> p (a d)"), ropec)
        B.dma("sp", sinT[:].rearrange("p a b c d -> p (a b c d)"), ropes)
        B.dma("pool", maskb[:].rearrange("p a d -> p (a d)"), masks)

        MEMSET("dve", Sel[:], 0.0)
        for c in range(4):
            MEMSET("dve", Sel[32 * c:32 * c + 1, c, :], 1.0)
        B.dma("sp", esC[:], sinkC)
        ACT(esC[:], esC[:], AF.Exp)
        CP("dve", GB5[:, 0:4, :], g4[:, 2:3, :].broadcast_to([128, 4, 64]))
        CP("dve", GB5[:, 4:5, :], g4[:, 3:4, :])
        cvf = cv[:].rearrange("p k t -> p (k t)")
        sg0 = SC(0, [16])
        ACT(sg0, cvf, AF.Sigmoid)
        TT("dve", scT[:].rearrange("p k t -> p (k t)"), sg0, cvf, ALU.mult)
        lp = SC(256, [2, 64])
        TT("dve", lp[:, 0, :], lam4[:, 0, :], lam4[:, 1, :], ALU.mult)
        TT("dve", lp[:, 1, :], lam4[:, 2, :], lam4[:, 3, :], ALU.mult)
        RED(smallc[:, 0:2], lp)
        ACT(smallc[:, 4:6], smallc[:, 0:2], AF.Exp)
        TT("dve", smallc[:, 2:3], smallc[:, 4:5], smallc[:, 5:6], ALU.subtract)
        TS("dve", smallc[:, 2:3], smallc[:, 2:3], LAMBDA_INIT, None, ALU.add)
        TS("dve", smallc[:, 3:4], smallc[:, 2:3], -1.0, None, ALU.mult)
        neglam = smallc[:, 3:4]
        ACT(esink[:], esink[:], AF.Exp)
        TS("dve", sublnG[:], sublnG[:], 1.0 - LAMBDA_INIT, None, ALU.mult)

        pm = PS[0]
        for j in range(12):
            s = load_cols(w_ada_v, j * 512, 512)
            wj = wv(s, 8, 512)
            for q in range(4):
                c = j * 4 + q
                for kc in range(8):
                    MM(pm[:, c * 2:c * 2 + 2], wj[:, kc, q * 128:(q + 1) * 128], scT[:, kc, :],
                       start=(kc == 0), stop=(kc == 7))
        TT("dve", modT[:], pm[:, 0:96].rearrange("p (c t) -> p c t", t=2),
           badaT[:].unsqueeze(2).broadcast_to([128, 48, 2]), ALU.add)
        for path in range(2):
            STT(AB[:, path, 0, :], modT[:, 8:16, path], 1.0, ngt[:, 0:8], ALU.add, ALU.mult)
            CP("dve", AB[:, path, 1, :], modT[:, 0:8, path])
            STT(AB[:, path, 2, :], modT[:, 32:40, path], 1.0, ngt[:, 8:16], ALU.add, ALU.mult)
            CP("dve", AB[:, path, 3, :], modT[:, 24:32, path])

        def make_gbc(path):
            for i, c0 in enumerate((16, 40)):
                for q4 in range(2):
                    bk = tpbank()
                    for q in range(4):
                        kc = q4 * 4 + q
                        rep = SC(1024 + (kc % 2) * 512, [128])
                        CP("dve", rep, modT[:, c0 + kc, path:path + 1].broadcast_to([128, 128]))
                        TR(PS[bk][:, q * 128:(q + 1) * 128], rep, ident_f[:])
                    CP("act", gbc[:, i, q4 * 512:(q4 + 1) * 512], PS[bk][:, :])

        def norm_to_T(src_tiles, dstT, tok0s, Acol, Bcol, from_dram):
            n = len(src_tiles)
            for g0 in range(0, n, 4):
                grp = list(range(g0, min(g0 + 4, n)))
                xnbs = []
                xts = []
                rss = []
                for gi, t in enumerate(grp):
                    if from_dram:
                        xt = SC(gi * 4096, [1024])
                        B.dma("sp", xt, src_tiles[t])
                    else:
                        xt = src_tiles[t]
                    junk = SC(24576, [1024])
                    ss = stat(1)
                    ACT(junk, xt, AF.Square, accum=ss)
                    rs = stat(1)
                    RSTD(rs, ss, 1024)
                    xts.append(xt)
                    rss.append(rs)
                for gi, t in enumerate(grp):
                    xnb = SC(16384 + gi * 2048, [1024], BF16)
                    ACT(xnb, xts[gi], AF.Identity, scale=rss[gi])
                    xnbs.append(xnb)
                ng = len(grp)
                for kc in range(8):
                    bk = tpbank()
                    pt = psb(bk, kc % 2)
                    for gi in range(ng):
                        TR(pt[:, gi * 128:(gi + 1) * 128], xnbs[gi][:, kc * 128:(kc + 1) * 128], ident_b[:])
                    contiguous = all(tok0s[grp[i]] == tok0s[grp[0]] + 128 * i for i in range(ng))
                    assert contiguous
                    dst = dstT[:, kc, tok0s[grp[0]]:tok0s[grp[0]] + 128 * ng]
                    if kc % 2 == 0:
                        TS("dve", dst, pt[:, 0:128 * ng], Acol[:, kc:kc + 1], Bcol[:, kc:kc + 1], ALU.mult, ALU.add)
                    else:
                        ACT(dst, pt[:, 0:128 * ng], AF.Identity, scale=Acol[:, kc:kc + 1], bias=Bcol[:, kc:kc + 1])

        def proj_tok(hT, tok0, wsl, ncols, coff=0):
            bk = nbank()
            out = PS[bk][:, 0:ncols]
            w = wsl
            for kc in range(8):
                MM(out, hT[:, kc, tok0:tok0 + 128], w[:, kc, coff:coff + ncols], start=(kc == 0), stop=(kc == 7))
            return out

        hn_ctr = [0]

        def headnorm(ps, ga, gb_, gain_ap, rope_tile, f32_out, bf_out, f32_cols=None):
            nh = ga * gb_
            w = nh * 64
            hn_ctr[0] += 1
            alt = hn_ctr[0] % 2 == 1
            o_sq, o_t0, o_xn, o_t1, o_t2 = (6400, 7680, 8960, 10240, 11520) if alt else (0, 1280, 2560, 3840, 5120)

            def v4(x):
                return x.rearrange("p (a b d) -> p a b d", a=ga, b=gb_, d=64)

            def v3(x):
                return x.rearrange("p (h d) -> p h d", d=64)
            sq = SC(o_sq, [320])[:, 0:w]
            ACT(sq, ps, AF.Square)
            xg = SC(o_t0, [320])[:, 0:w]
            TT("dve", v4(xg), v4(ps), gain_ap, ALU.mult)
            ss = stat(nh)
            RED(ss, v3(sq))
            rs = stat(nh)
            RSTD(rs, ss, 64)
            rsb = rs.unsqueeze(2).broadcast_to([128, nh, 64])
            if rope_tile is None:
                TT("dve", v3(bf_out), v3(xg), rsb, ALU.mult)
                if f32_out is not None:
                    lo, hi = f32_cols if f32_cols is not None else (0, w)
                    TT("dve", v3(f32_out[:, lo:hi]), v3(xg[:, lo:hi]),
                       rs[:, lo // 64:hi // 64].unsqueeze(2).broadcast_to([128, (hi - lo) // 64, 64]), ALU.mult)
                return
            t1 = SC(o_t1, [320])[:, 0:w]
            t2 = SC(o_t2, [320])[:, 0:w]
            cosb = cosT[:, rope_tile, :].unsqueeze(1).broadcast_to([128, nh, 64])
            TT("dve", v3(t1), v3(xg), cosb, ALU.mult)
            xn5 = xg.rearrange("p (h a b c) -> p h a b c", a=2, b=2, c=16)
            t25 = t2.rearrange("p (h a b c) -> p h a b c", a=2, b=2, c=16)
            nsin = sinT[:, rope_tile, 0, :, :].unsqueeze(1).broadcast_to([128, nh, 2, 16])
            psin = sinT[:, rope_tile, 1, :, :].unsqueeze(1).broadcast_to([128, nh, 2, 16])
            TT("pool", t25[:, :, :, 0, :], xn5[:, :, :, 1, :], nsin, ALU.mult)
            TT("pool", t25[:, :, :, 1, :], xn5[:, :, :, 0, :], psin, ALU.mult)
            sm = SC(o_xn, [320])[:, 0:w]
            TT("dve", sm, t1, t2, ALU.add)
            TT("dve", v3(bf_out), v3(sm), rsb, ALU.mult)

        ost_ctr = [0]

        def ostage(w=512):
            i = ost_ctr[0]
            ost_ctr[0] = i + 1
            return SC(24576 + (i % 3) * 2048, [512])[:, 0:w]

        qkb_ctr = [0]

        def qkbuf(w=512):
            i = qkb_ctr[0]
            qkb_ctr[0] = i + 1
            return SC(30720 + (i % 2) * 1024, [512], BF16)[:, 0:w]

        pt_ctr = [0]

        def ptbuf():
            i = pt_ctr[0]
            pt_ctr[0] = i + 1
            return SC(32768 + (i % 4) * 1024, [512], BF16)

        OFF_HT = 0
        OFF_U = 32 * 1024
        OFF_OAT = 57 * 1024
        OFF_OBT = 73 * 1024
        OFF_X1 = 57 * 1024
        OFF_ACT = 0
        OFF_H2T = 89 * 1024
        OFF_MRG = 32 * 1024

        def run_pass(path):
            sample = (path == 1)
            xd = xs if sample else xp
            ydst = ys if sample else yp
            NTA = 16 if sample else 8
            T = NTA * 128
            hT = AR(OFF_HT, [8, T])
            oaT = AR(OFF_OAT, [8, 1024])
            obT = AR(OFF_OBT, [8, 1024])
            Acol1, Bcol1 = AB[:, path, 0, :], AB[:, path, 1, :]
            Acol2, Bcol2 = AB[:, path, 2, :], AB[:, path, 3, :]

            make_gbc(path)

            norm_to_T([xd[t * 128:(t + 1) * 128, :] for t in range(NTA)], hT,
                      [t * 128 for t in range(NTA)], Acol1, Bcol1, True)

            USZ = 12800
            NKA = 2560 if sample else 1024
            NKB = 1664 if sample else 1024
            FIN = OFF_H2T

            def FB(off, shape, dt=F32):
                return AR(FIN + off, shape, dt)

            mq = []

            def mq_tick(n=1):
                for _ in range(n):
                    if mq:
                        mq.pop(0)[1]()

            def mq_flush(upto=None):
                while mq and (upto is None or mq[0][0] <= upto):
                    mq.pop(0)[1]()

            def projA(head, uoff, banks=(6,)):
                W = wv(WS.load(), 8, 384)
                qT = AR(uoff, [1024])
                kT = AR(uoff + 2048, [NKA])
                Va = AR(uoff + 2048 + NKA * 2, [NKA // 128, 128])
                gain = g4[:, 0:2, :].unsqueeze(2).broadcast_to([128, 2, 2, 64])
                gain_k = g4[:, 1:2, :].unsqueeze(2).broadcast_to([128, 1, 2, 64])
                if sample:
                    kc_b = SC(12800, [4, 128], F32)
                    B.dma("sp", kc_b, cdk.rearrange("(t p) n -> p t n", p=128)[:, :, head * 128:(head + 1) * 128])
                    B.dma("pool", Va[:, 16:20, :], cdv.rearrange("(t p) n -> p t n", p=128)[:, :, head * 128:(head + 1) * 128])
                    pt = PS[7][:, :]
                    for t in range(4):
                        TR(pt[:, t * 128:(t + 1) * 128], kc_b[:, t, :], ident_f[:])
                    CP("act", kT[:, 2048:2560], pt[:, 0:512])
                    yield
                pend = []
                for t in range(NTA):
                    rope_tile = t if sample else None
                    pbk = PS[banks[t % len(banks)]]
                    if t < 8:
                        ps = pbk[:, 0:384]
                        for kc in range(8):
                            MM(ps, hT[:, kc, t * 128:(t + 1) * 128], W[:, kc, 0:384], start=(kc == 0), stop=(kc == 7))
                        qk = qkbuf(256)
                        f32o = ostage(256) if not sample else None
                        headnorm(ps[:, 0:256], 2, 2, gain, rope_tile, f32o, qk, f32_cols=(128, 256))
                        if not sample:
                            B.dma("sp", ndk[t * 128:(t + 1) * 128, head * 128:(head + 1) * 128], f32o[:, 128:256])
                        vsl = ps[:, 256:384]

                        def post(t=t, qk=qk):
                            pt = psb(7, t % 2)
                            for c in range(2):
                                TR(pt[:, c * 128:(c + 1) * 128], qk[:, c * 128:(c + 1) * 128], ident_b[:])
                            CP("act", qT[:, t * 128:(t + 1) * 128], pt[:, 0:128])
                            CP("dve" if sample else "act", kT[:, t * 128:(t + 1) * 128], pt[:, 128:256])
                    else:
                        ps = pbk[:, 0:256]
                        for kc in range(8):
                            MM(ps, hT[:, kc, t * 128:(t + 1) * 128], W[:, kc, 128:384], start=(kc == 0), stop=(kc == 7))
                        qk = qkbuf(128)
                        headnorm(ps[:, 0:128], 1, 2, gain_k, rope_tile, None, qk)
                        vsl = ps[:, 128:256]

                        def post(t=t, qk=qk):
                            pt = psb(7, t % 2)
                            TR(pt[:, 0:128], qk[:, 0:128], ident_b[:])
                            CP("act", kT[:, t * 128:(t + 1) * 128], pt[:, 0:128])
                    CP("act", Va[:, t, :], vsl)
                    if not sample:
                        vo = ostage(128)
                        CP("act", vo, vsl)
                        B.dma("sp", ndv[t * 128:(t + 1) * 128, head * 128:(head + 1) * 128], vo)
                    while pend:
                        pend.pop(0)()
                    pend.append(post)
                    yield
                while pend:
                    pend.pop(0)()
                yield

            def attnA(head, uoff):
                qT = AR(uoff, [1024])
                kT = AR(uoff + 2048, [NKA])
                Va = AR(uoff + 2048 + NKA * 2, [NKA // 128, 128])
                if sample:
                    jobs = [[(qb * 512, 512, [(c * 128, c) for c in range(20)])] for qb in range(2)]
                else:
                    jobs = [[(sq_ * 256, 256, [(sq_ * 256 + c * 128, sq_ * 2 + c) for c in range(2)])
                             for sq_ in (2 * jb, 2 * jb + 1)] for jb in range(2)]
                OT = [PS[2], PS[3]]
                DENC = PS[4]
                RB = PS[5]
                for job in jobs:
                    Q0 = job[0][0]
                    steps = []
                    for (q0, nq, chunks) in job:
                        for ci, (koff, vch) in enumerate(chunks):
                            steps.append((q0, nq, koff, vch, ci == 0, ci == len(chunks) - 1))

                    def qk_exp(st):
                        q0, nq, koff, vch, first, last = st
                        pts = []
                        for sub in range(2):
                            S = PS[sub][:, 0:nq]
                            MM(S, kT[sub * 64:(sub + 1) * 64, koff:koff + 128], qT[sub * 64:(sub + 1) * 64, q0:q0 + nq])
                            P = ptbuf()[:, 0:nq]
                            ACT(P, S, AF.Exp, scale=0.125)
                            pts.append(P)
                        return pts

                    def pv_den(st, pts):
                        q0, nq, koff, vch, first, last = st
                        c0 = q0 - Q0
                        for sub in range(2):
                            MM(OT[sub][:, c0:c0 + nq], Va[:, vch, :], pts[sub], start=first, stop=last)
                        for sub in range(2):
                            for j in range(nq // 128):
                                c = c0 // 128 + j
                                MMT(DENC[32 * c:32 * c + 32, sub * 128:(sub + 1) * 128], ones_b[:, 0:32],
                                    pts[sub][:, j * 128:(j + 1) * 128], first and sub == 0, last, (0, 32 * c))

                    prev = None
                    for i, st in enumerate(steps):
                        pts = qk_exp(st)
                        if prev is not None:
                            pv_den(*prev)
                        prev = (st, pts)
                        mq_tick(1)
                        yield
                    pv_den(*prev)
                    tag = a_ctr[0]
                    a_ctr[0] += 1
                    mq_flush()
                    fbase = FIN

                    def FB(off, shape, dt=F32, fbase=fbase):
                        return AR(fbase + off, shape, dt)
                    o0 = FB(0, [512])
                    o1s = FB(2048, [512])
                    tt_ = FB(8192, [512])
                    uu = FB(10240, [512])
                    oa = FB(12288, [512])
                    sqb = FB(14336, [512], BF16)
                    dcc = FB(4096, [256])
                    CP("dve", dcc, DENC[:, 0:256])
                    CP("dve" if sample else "act", o0, OT[0][:, :])
                    CP("dve" if sample else "act", o1s, OT[1][:, :])
                    tasks = []
                    tasks.append(lambda: RECIP(dcc, dcc))

                    def bcast(sub):
                        for c in range(4):
                            MM(RB[:, c * 128:(c + 1) * 128], Sel[:, c, :], dcc[:, sub * 128:(sub + 1) * 128], start=True, stop=True)
                    tasks.append(lambda: bcast(0))
                    tasks.append(lambda: TT("dve", tt_, o0, RB[:, :], ALU.mult))
                    tasks.append(lambda: bcast(1))
                    tasks.append(lambda: TT("dve", uu, o1s, RB[:, :], ALU.mult))
                    tasks.append(lambda: STT(oa, uu, neglam, tt_, ALU.mult, ALU.add))
                    tasks.append(lambda: TT("dve", sqb, oa, oa, ALU.mult))
                    holder = []

                    def part2a(sqb=sqb, holder=holder, FB=FB):
                        for qt in range(4):
                            MM(PS[7][:, qt:qt + 1], sqb[:, qt * 128:(qt + 1) * 128], ones_b[:, 0:1], start=True, stop=True)
                        rsq = stat(4)
                        TS("dve", rsq, PS[7][:, 0:4], 1.0 / 128, EPS, ALU.mult, ALU.add)
                        TT("pool", rsq, rsq, negh[:, 0:4], ALU.pow)
                        for qt in range(4):
                            rep = FB(qt * 512, [128])
                            CP("dve", rep, rsq[:, qt:qt + 1].broadcast_to([128, 128]))
                            holder.append(rep)

                    def part2b(holder=holder, Q0=Q0, oa=oa):
                        for qt in range(4):
                            TR(PS[7][:, qt * 128:(qt + 1) * 128], holder[qt], ident_f[:])
                        STT(oaT[:, head, Q0:Q0 + 512], oa, sublnC[:, 0:1], PS[7][:, :], ALU.mult, ALU.mult)
                    tasks.append(lambda: None)
                    tasks.append(part2a)
                    tasks.append(lambda: None)
                    tasks.append(lambda: None)
                    tasks.append(part2b)
                    for tk in tasks:
                        mq.append((tag, tk))
                    yield
                yield

            def projB(g4i, uoff):
                W = wv(WS.load(), 8, 384)
                qT = AR(uoff, [2, 1024])
                kT = AR(uoff + 4096, [NKB])
                Vb = AR(uoff + 4096 + NKB * 2, [NKB // 128, 2, 64])
                gk = g4[:, 3:4, :].unsqueeze(2).broadcast_to([128, 1, 1, 64])
                g5 = GB5[:].unsqueeze(1)
                NTB = 9 if sample else 8
                if sample:
                    kc_b = SC(12800, [4, 2, 64], F32)
                    src = cwk.rearrange("(t p) n -> p t n", p=128)[:, :, g4i * 64:(g4i + 1) * 64]
                    srcv = cwv.rearrange("(t p) n -> p t n", p=128)[:, :, g4i * 64:(g4i + 1) * 64]
                    for dup in range(2):
                        B.dma("sp", kc_b[:, :, dup, :], src)
                        B.dma("pool", Vb[:, 9:13, dup, :], srcv)
                    pt = PS[7][:, :]
                    for t in range(4):
                        TR(pt[:, t * 128:(t + 1) * 128], kc_b[:, t, :, :].rearrange("p a d -> p (a d)"), ident_f[:])
                    CP("act", kT[:, 1152:1664], pt[:, 0:512])
                    yield
                pend = []
                for t in range(NTB):
                    rope_tile = t if sample else None
                    posts = []
                    if t < 8:
                        ps = PS[6][:, 0:384]
                        for kc in range(8):
                            MM(ps, hT[:, kc, t * 128:(t + 1) * 128], W[:, kc, 0:384], start=(kc == 0), stop=(kc == 7))
                        qb_ = qkbuf(320)
                        f32o = ostage(320) if not sample else None
                        headnorm(ps[:, 0:320], 1, 5, g5, rope_tile, f32o, qb_, f32_cols=(256, 320))
                        kb_ = qb_[:, 256:320]
                        if not sample:
                            B.dma("sp", nwk[t * 128:(t + 1) * 128, g4i * 64:(g4i + 1) * 64], f32o[:, 256:320])
                        vsl = ps[:, 320:384]

                        def post1(t=t, qb_=qb_):
                            pt = psb(7, t % 2)
                            for c in range(2):
                                TR(pt[:, c * 128:(c + 1) * 128], qb_[:, c * 128:(c + 1) * 128], ident_b[:])
                            CP("act", qT[:, :, t * 128:(t + 1) * 128], pt[:, 0:256].rearrange("p (c k) -> p c k", c=2))
                        posts.append(post1)
                    else:
                        ps = PS[6][:, 0:128]
                        for kc in range(8):
                            MM(ps, hT[:, kc, t * 128:(t + 1) * 128], W[:, kc, 256:384], start=(kc == 0), stop=(kc == 7))
                        kb_ = SC(15872, [64], BF16)
                        headnorm(ps[:, 0:64], 1, 1, gk, rope_tile, None, kb_)
                        vsl = ps[:, 64:128]
                    CP("act", Vb[:, t, 0, :], vsl)
                    CP("dve" if sample else "act", Vb[:, t, 1, :], vsl)
                    if not sample:
                        vo = ostage(64)
                        CP("act", vo, vsl)
                        B.dma("sp", nwv[t * 128:(t + 1) * 128, g4i * 64:(g4i + 1) * 64], vo)

                    kd = SC(15360 + (t % 2) * 256, [2, 64], BF16)
                    CP("dve", kd[:, 0, :], kb_)
                    CP("dve", kd[:, 1, :], kb_)

                    def post2(t=t, kd=kd):
                        pt = psb(7, t % 2)
                        TR(pt[:, 256:384], kd.rearrange("p a d -> p (a d)"), ident_b[:])
                        CP("dve", kT[:, t * 128:(t + 1) * 128], pt[:, 256:384])
                    posts.append(post2)
                    while pend:
                        pend.pop(0)()
                    pend.extend(posts)
                    yield
                while pend:
                    pend.pop(0)()
                yield

            def attnB(g4i, uoff):
                qT = AR(uoff, [2, 1024])
                kT = AR(uoff + 4096, [NKB])
                Vb = AR(uoff + 4096 + NKB * 2, [NKB // 128, 2, 64])
                for qt in range(8):
                    if sample:
                        chunks = []
                        if qt == 0:
                            chunks.append((8 * 128, 8, 2))
                        else:
                            chunks.append(((qt - 1) * 128, qt - 1, 0))
                        chunks.append((qt * 128, qt, None))
                        if qt == 7:
                            chunks.append((8 * 128, 8, 3))
                        else:
                            chunks.append(((qt + 1) * 128, qt + 1, 1))
                        chunks += [(1152 + c * 128, 9 + c, None) for c in range(4)]
                    else:
                        s0 = (qt // 2) * 2
                        chunks = [((s0 + c) * 128, s0 + c, None) for c in range(2)]
                    jb = b_ctr[0]
                    b_ctr[0] += 1
                    mq_flush(upto=1000000 + jb - 2)
                    OTb = PS[2 + (jb % 2)]
                    DENb = PS[4 + (jb % 2)]

                    def qk_exp(ch):
                        koff, vch, mk = ch
                        P = ptbuf()
                        for par in range(2):
                            S = PS[par][:, 0:256]
                            MM(S.rearrange("p (a q) -> p a q", a=2), kT[par * 64:(par + 1) * 64, koff:koff + 128],
                               qT[par * 64:(par + 1) * 64, :, qt * 128:(qt + 1) * 128])
                            ACT(P[:, par * 256:(par + 1) * 256], S, AF.Exp, scale=0.125)
                        if mk is not None:
                            P4 = P.rearrange("p (a q) -> p a q", a=4)
                            TT("dve", P4, P4, maskb[:, mk:mk + 1, :].broadcast_to([128, 4, 128]), ALU.mult)
                        return P

                    def pv_den(ci, ch, P):
                        first = (ci == 0)
                        last = (ci == len(chunks) - 1)
                        MM(OTb[:, :], Vb[:, ch[1], :, :].rearrange("p a d -> p (a d)"), P, start=first, stop=last)
                        for c in range(4):
                            MMT(DENb[32 * c:32 * c + 32, 0:128], ones_b[:, 0:32], P[:, c * 128:(c + 1) * 128], first, last, (0, 32 * c))

                    prev = None
                    for ci, ch in enumerate(chunks):
                        P = qk_exp(ch)
                        if prev is not None:
                            pv_den(*prev)
                        prev = (ci, ch, P)
                        mq_tick(1)
                        yield
                    pv_den(*prev)
                    dn = SC(16384 + (jb % 2) * 2048, [512])
                    dnc = SC(20480 + (jb % 2) * 512, [128])
                    TS("dve", dnc, DENb[:, 0:128], esC[:, g4i:g4i + 1], None, ALU.add)
                    c0 = g4i * 2
                    tagb = 1000000 + jb
                    mq.append((tagb, lambda dnc=dnc: RECIP(dnc, dnc)))

                    def bcastb(dnc=dnc, DENb=DENb, dn=dn):
                        for c in range(4):
                            MM(DENb[:, c * 128:(c + 1) * 128], Sel[:, c, :], dnc, start=True, stop=True)
                        CP("dve", dn, DENb[:, :])
                    mq.append((tagb, bcastb))

                    def fmul(par, dn=dn, OTb=OTb, c0=c0, qt=qt):
                        ps_ = slice(par * 64, (par + 1) * 64)
                        cs_ = slice(par * 256, (par + 1) * 256)
                        TT("dve", obT[ps_, c0:c0 + 2, qt * 128:(qt + 1) * 128],
                           OTb[ps_, cs_].rearrange("p (a q) -> p a q", a=2),
                           dn[ps_, cs_].rearrange("p (a q) -> p a q", a=2), ALU.mult)
                    mq.append((tagb, lambda fmul=fmul: fmul(0)))
                    mq.append((tagb, lambda fmul=fmul: fmul(1)))
                    yield

            groups = [("A", h) for h in range(8)] + [("B", g) for g in range(4)]
            gens = []
            for gi, (kind, idx) in enumerate(groups):
                uoff = OFF_U + (gi % 2) * USZ
                if kind == "A":
                    gens.append((projA(idx, uoff, banks=((6, 0, 1, 2, 3) if gi == 0 else (6,))), attnA(idx, uoff), (2 if sample else 1)))
                else:
                    gens.append((projB(idx, uoff), attnB(idx, uoff), (5 if sample else 2)))
            for _ in gens[0][0]:
                pass
            for k in range(len(gens)):
                a = gens[k][1]
                ratio = gens[k][2]
                p = gens[k + 1][0] if k + 1 < len(gens) else None
                cnt = 0
                for _ in a:
                    cnt += 1
                    if p is not None and cnt % ratio == 0:
                        try:
                            next(p)
                        except StopIteration:
                            p = None
                if p is not None:
                    for _ in p:
                        pass
            mq_flush()

            mrg = AR(OFF_MRG, [8, 1024])
            for n in range(8):
                ws_ = WS.load()
                W = wv(ws_, 8, 512)
                for tb in range(2):
                    tk = slice(tb * 512, (tb + 1) * 512)
                    pb = [PS[nbank(0, 8)] for _ in range(4)]
                    srcs = [oaT, obT, hT, hT]
                    for i in range(4):
                        for kc in range(8):
                            MM(pb[i][:, :], W[:, kc, i * 128:(i + 1) * 128], srcs[i][:, kc, tk], start=(kc == 0), stop=(kc == 7))
                    sb0 = 0 if (n * 2 + tb) % 2 == 0 else 16384
                    sga = SC(sb0, [512])
                    sgb = SC(sb0 + 2048, [512])
                    ACT(sga, pb[2][:, :], AF.Sigmoid)
                    ACT(sgb, pb[3][:, :], AF.Sigmoid)
                    m1 = SC(sb0 + 4096, [512])
                    m2 = SC(sb0 + 6144, [512])
                    TT("dve", m1, sga, pb[0][:, :], ALU.mult)
                    TT("dve", m2, sgb, pb[1][:, :], ALU.mult)
                    TT("pool", mrg[:, n, tk], m1, m2, ALU.add)

            x1 = AR(OFF_X1, [8, 1024], F32)
            for nh in range(2):
                W = wv(load_cols(w_out_v, nh * 512, 512), 8, 512)
                for t in range(8):
                    bk = nbank(0, 8)
                    o = PS[bk][:, :]
                    for kc in range(8):
                        MM(o, mrg[:, kc, t * 128:(t + 1) * 128], W[:, kc, :], start=(kc == 0), stop=(kc == 7))
                    xr = SC(8192 + (t % 3) * 2048, [512])
                    B.dma("sp", xr, xd[t * 128:(t + 1) * 128, nh * 512:(nh + 1) * 512])
                    tmp = SC((t % 2) * 2048, [512])
                    TT("dve", tmp, o, gbc[:, 0, nh * 512:(nh + 1) * 512], ALU.mult)
                    TT("pool" if t % 2 else "dve", x1[:, t, nh * 512:(nh + 1) * 512], tmp, xr, ALU.add)

            h2T = AR(OFF_H2T, [8, 1024])
            norm_to_T([x1[:, t, :] for t in range(8)], h2T, [t * 128 for t in range(8)], Acol2, Bcol2, False)
            actT = AR(OFF_ACT, [NFF, 1024])
            for fg in range(6):
                nc_ = 512 if fg < 5 else 256
                Wg = wv(load_cols(w_gate_v, fg * 512, nc_), 8, nc_)
                Wu = wv(load_cols(w_up_v, fg * 512, nc_), 8, nc_)
                for j in range(nc_ // 128):
                    f = fg * 4 + j
                    for tb in range(2):
                        tk = slice(tb * 512, (tb + 1) * 512)
                        pg = PS[nbank(0, 8)]
                        pu = PS[nbank(0, 8)]
                        for kc in range(8):
                            MM(pg[:, :], Wg[:, kc, j * 128:(j + 1) * 128], h2T[:, kc, tk], start=(kc == 0), stop=(kc == 7))
                        for kc in range(8):
                            MM(pu[:, :], Wu[:, kc, j * 128:(j + 1) * 128], h2T[:, kc, tk], start=(kc == 0), stop=(kc == 7))
                        sg = SC(((f * 2 + tb) % 2) * 2048, [512])
                        ACT(sg, pg[:, :], AF.Silu)
                        TT("dve", actT[:, f, tk], sg, pu[:, :], ALU.mult)
            for nh in range(2):
                for fg in range(6):
                    nf = 4 if fg < 5 else 2
                    Wd = wv(WS.load(), nf, 512)
                    for j in range(nf):
                        f = fg * 4 + j
                        for t in range(8):
                            MM(PS[t][:, :], actT[:, f, t * 128:(t + 1) * 128], Wd[:, j, :], start=(f == 0), stop=(f == NFF - 1))
                for t in range(8):
                    tmp = SC(4096 + (t % 2) * 2048, [512])
                    TT("dve", tmp, PS[t][:, :], gbc[:, 1, nh * 512:(nh + 1) * 512], ALU.mult)
                    yo = ostage()
                    TT("pool" if t % 2 else "dve", yo, tmp, x1[:, t, nh * 512:(nh + 1) * 512], ALU.add)
                    B.dma("sp", ydst[t * 128:(t + 1) * 128, nh * 512:(nh + 1) * 512], yo)

        run_pass(0)
        run_pass(1)
        B.emit()
        print("ops:", B.nops, {e: len(s) for e, s in B.streams.items()})
    return nc


_PROGRAM = None


def _rope_tables(pos):
    quarter = 16
    freqs = (10000.0 ** (-np.arange(quarter, dtype=np.float32) / quarter)).astype(np.float32)
    row = (pos // 64).astype(np.float32)
    col = (pos % 64).astype(np.float32)
    ang_r = row[:, None] * freqs[None, :]
    ang_c = col[:, None] * freqs[None, :]
    cr, sr = np.cos(ang_r).astype(np.float32), np.sin(ang_r).astype(np.float32)
    cc, sc = np.cos(ang_c).astype(np.float32), np.sin(ang_c).astype(np.float32)
    cos = np.concatenate([cr, cr, cc, cc], axis=1)
    sin = np.stack([sr, sc], axis=1)
    sins = np.stack([-sin, sin], axis=1)
    return cos, sins.reshape(len(pos), 64)


def _to_tiles(a, nt):
    w = a.shape[1]
    return np.ascontiguousarray(a.reshape(nt, 128, w).transpose(1, 0, 2).reshape(128, nt * w))


def kernel(x_prompt, x_sample, cache_diff_k, cache_diff_v, cache_win_k, cache_win_v, c, c_ctx,
           w_ada, b_ada, norm1_g, w_in, qn_a, kn_a, lambda_q1, lambda_k1, lambda_q2, lambda_k2,
           subln_g, qn_b, kn_b, sink, w_oa, w_ob, w_out, norm2_g, w_gate, w_up, w_down):
    global _PROGRAM
    f = lambda a: np.ascontiguousarray(np.asarray(a, dtype=np.float32))
    x_prompt, x_sample = f(x_prompt), f(x_sample)
    if _PROGRAM is None:
        _PROGRAM = build_program()
    nc = _PROGRAM

    def featT(v, k):
        return np.ascontiguousarray(f(v).reshape(k, 128).T)

    shared = {
        "w_ada": f(w_ada[0]), "b_adaT": featT(b_ada[0], 48),
        "ngT": np.ascontiguousarray(np.concatenate([featT(norm1_g[0], 8), featT(norm2_g[0], 8)], axis=1)),
        "w_in": f(w_in[0]),
        "gains": np.concatenate([f(qn_a[0]), f(kn_a[0]), f(qn_b[0]), f(kn_b[0])]),
        "lams": np.concatenate([f(lambda_q1[0]), f(lambda_k1[0]), f(lambda_q2[0]), f(lambda_k2[0])]),
        "subln": f(subln_g[0]), "sink": f(sink[0]),
        "sinkC": np.ascontiguousarray(np.stack([np.repeat(np.stack([f(sink[0])[g * 4 + ((c % 2) * 2 + c // 2)] for c in range(4)]), 32)
                                                for g in range(4)], axis=1)),
        "w_oa": f(w_oa[0]), "w_ob": f(w_ob[0]), "w_out": f(w_out[0]),
        "w_gate": f(w_gate[0]), "w_up": f(w_up[0]), "w_down": f(w_down[0]),
        "ident": np.eye(128, dtype=np.float32),
    }
    j = np.arange(128)[:, None]
    q = np.arange(128)[None, :]
    triL = (j >= q).astype(np.float32)
    triR = (j <= q).astype(np.float32)
    in_maps = []
    for core in range(8):
        b, hf = core // 2, core % 2
        own = np.arange(hf * 1024, (hf + 1) * 1024)
        if hf == 0:
            oth = np.arange(1024, 2048)
        else:
            oth = np.concatenate([np.arange(896, 1024), np.arange(0, 896)])
        order = np.concatenate([own, oth])
        cos, sins = _rope_tables(order)
        m = np.stack([triL, triR, triL * float(hf), triR * float(1 - hf)], axis=1).reshape(128, 512)
        cvec = np.stack([featT(c_ctx, 8), featT(c[b], 8)], axis=2).reshape(128, 16)
        d = dict(shared)
        d.update({
            "xp": x_prompt[core * 4:(core + 1) * 4].reshape(1024, 1024),
            "xs": np.ascontiguousarray(x_sample[b][order]),
            "cdk": f(cache_diff_k[b, 0]).reshape(512, 1024),
            "cdv": f(cache_diff_v[b, 0]).reshape(512, 1024),
            "cwk": f(cache_win_k[b, 0]).reshape(512, 256),
            "cwv": f(cache_win_v[b, 0]).reshape(512, 256),
            "cvec": np.ascontiguousarray(cvec),
            "ropec": _to_tiles(cos, 16), "ropes": _to_tiles(sins, 16),
            "masks": np.ascontiguousarray(m),
        })
        in_maps.append(d)
    res = run_bass_kernel_spmd(nc, in_maps, core_ids=list(range(8)))
    R = res.results
    y_p = np.concatenate([r["yp"].reshape(4, 256, 1024) for r in R], axis=0)
    y_s = np.stack([np.concatenate([R[2 * b]["ys"], R[2 * b + 1]["ys"]], axis=0) for b in range(4)], axis=0)
    n_dk = np.concatenate([r["ndk"].reshape(4, 1, 256, 8, 2, 64) for r in R], axis=0)
    n_dv = np.concatenate([r["ndv"].reshape(4, 1, 256, 8, 128) for r in R], axis=0)
    n_wk = np.concatenate([r["nwk"].reshape(4, 1, 256, 4, 64) for r in R], axis=0)
    n_wv = np.concatenate([r["nwv"].reshape(4, 1, 256, 4, 64) for r in R], axis=0)
    return (y_p.astype(np.float32), y_s.astype(np.float32), n_dk.astype(np.float32),
            n_dv.astype(np.float32), n_wk.astype(np.float32), n_wv.astype(np.float32))
```

```python
import contextlib
import numpy as np
import concourse.bass as bass
import concourse.mybir as mybir
from concourse.bass_utils import run_bass_kernel_spmd

F32 = mybir.dt.float32
BF16 = mybir.dt.bfloat16
AF = mybir.ActivationFunctionType
ALU = mybir.AluOpType
AX = mybir.AxisListType

_DT_SIZE = {mybir.dt.float32: 4, mybir.dt.bfloat16: 2}
N_DMA_SLOTS = 8
EPS = 1e-6


class _Op:
    __slots__ = ("eng", "fn", "deps", "needs_inc", "val", "dma", "slot", "dval", "attach")

    def __init__(self, eng, fn, dma=False):
        self.eng = eng
        self.fn = fn
        self.attach = (eng != "pe")
        self.deps = []
        self.needs_inc = False
        self.val = None
        self.dma = dma
        self.slot = None
        self.dval = None


def _region(ap):
    t = ap.tensor
    space = str(ap.space)
    if space == "PSUM":
        return (t.name, 0, 128, 0, 1 << 30, True, True)
    esz = _DT_SIZE[ap.dtype]
    dims = list(ap.ap)
    if space == "SB":
        pstep, pcount = dims[0]
        if pstep == 0:
            pstep = 1 << 40
        p0 = ap.offset // pstep
        f0 = ap.offset % pstep
        rest = dims[1:]
    else:
        p0, pcount, f0, rest = 0, 1, ap.offset, dims
    ext = 0
    n = 1
    for s, c in rest:
        ext += (c - 1) * abs(s)
        if s != 0:
            n *= c
    lo = f0 * esz
    hi = (f0 + ext + 1) * esz
    return (t.name, p0, p0 + pcount, lo, hi, n == ext + 1, False)


class Builder:
    ENG = ("pe", "act", "dve", "pool", "sp")

    def __init__(self, nc, ro_dram=()):
        self.nc = nc
        self.streams = {e: [] for e in self.ENG}
        self.recs = {}
        self.dma_count = {e: 0 for e in self.ENG}
        self.ro_dram = set(ro_dram)
        self.nops = 0

    def _track(self, op, reads, writes):
        deps = []
        acc = []
        for ap in reads:
            if str(ap.space) == "DRAM" and ap.tensor.name in self.ro_dram:
                continue
            acc.append((_region(ap), "r"))
        for ap in writes:
            acc.append((_region(ap), "w"))
        for (name, p0, p1, lo, hi, dense, psum), k in acc:
            lst = self.recs.setdefault(name, [])
            keep = []
            for r in lst:
                if r[0] < p1 and p0 < r[1] and r[2] < hi and lo < r[3]:
                    d = r[4]
                    if d is not op and (r[5] == "w" or k == "w" or psum):
                        if d.eng == op.eng and not d.dma and not op.dma:
                            if op.eng != "pe" and r[5] == "w":
                                deps.append(d)
                        else:
                            deps.append(d)
                    if (k == "w" or psum) and dense and p0 <= r[0] and r[1] <= p1 and lo <= r[2] and r[3] <= hi:
                        continue
                keep.append(r)
            lst[:] = keep
        for (name, p0, p1, lo, hi, dense, psum), k in acc:
            lst = self.recs[name]
            if k == "r" and not op.dma:
                for r in lst:
                    if r[5] == "r" and r[0] == p0 and r[1] == p1 and r[2] == lo and r[3] == hi \
                            and r[4].eng == op.eng and not r[4].dma:
                        r[4] = op
                        break
                else:
                    lst.append([p0, p1, lo, hi, op, k])
            else:
                lst.append([p0, p1, lo, hi, op, k])
        seen = set()
        for d in deps:
            if id(d) in seen:
                continue
            seen.add(id(d))
            d.needs_inc = True
            op.deps.append(d)

    def op(self, eng, fn, reads=(), writes=(), attach=None):
        o = _Op(eng, fn)
        if attach is not None:
            o.attach = attach and eng != "pe"
        self._track(o, list(reads), list(writes))
        self.streams[eng].append(o)
        self.nops += 1
        return o

    def dma(self, q, out, in_):
        o = _Op(q, lambda e: e.dma_start(out=out, in_=in_), dma=True)
        j = self.dma_count[q]
        self.dma_count[q] = j + 1
        o.slot = j % N_DMA_SLOTS
        o.dval = 16 * (j // N_DMA_SLOTS + 1)
        self._track(o, [in_], [out])
        self.streams[q].append(o)
        self.nops += 1
        return o

    def emit(self):
        nc = self.nc
        with contextlib.ExitStack() as es:
            esem = {e: es.enter_context(nc.semaphore(f"s_{e}")) for e in self.ENG}
            dsem = {e: [es.enter_context(nc.semaphore(f"d_{e}{i}")) for i in range(N_DMA_SLOTS)]
                    for e in self.ENG if self.dma_count[e] > 0}
            for e in self.ENG:
                c = 0
                for o in self.streams[e]:
                    if not o.dma and o.needs_inc:
                        c += 1
                        o.val = c
            block = es.enter_context(nc.Block())
            regs = {"pe": block.tensor, "act": block.scalar, "dve": block.vector,
                    "pool": block.gpsimd, "sp": block.sync}

            def make(e):
                stream = self.streams[e]

                def body(eng):
                    waited = {}

                    def wait(sem, val):
                        k = id(sem)
                        if waited.get(k, 0) < val:
                            eng.wait_ge(sem, val)
                            waited[k] = val
                    for o in stream:
                        need = {}
                        for d in o.deps:
                            sem, val = (dsem[d.eng][d.slot], d.dval) if d.dma else (esem[d.eng], d.val)
                            if need.get(id(sem), (None, 0))[1] < val:
                                need[id(sem)] = (sem, val)
                        if o.dma and o.dval > 16:
                            sem, val = dsem[e][o.slot], o.dval - 16
                            if need.get(id(sem), (None, 0))[1] < val:
                                need[id(sem)] = (sem, val)
                        waits = [(sem, val) for k, (sem, val) in need.items() if waited.get(k, 0) < val]
                        for sem, val in waits:
                            waited[id(sem)] = val
                        att = waits.pop() if (waits and o.attach) else None
                        for sem, val in waits:
                            eng.wait_ge(sem, val)
                        ins = o.fn(eng)
                        if att is not None:
                            ins.wait_op(att[0], att[1], "sem-ge")
                        if o.dma:
                            ins.then_inc(dsem[e][o.slot], 16)
                        elif o.needs_inc:
                            ins.then_inc(esem[e], 1)
                    n = self.dma_count[e]
                    if n > 0:
                        for s in range(N_DMA_SLOTS):
                            cnt = (n - s + N_DMA_SLOTS - 1) // N_DMA_SLOTS
                            if cnt > 0:
                                wait(dsem[e][s], 16 * cnt)
                return body
            for e in self.ENG:
                if self.streams[e]:
                    regs[e](make(e))


D = 1024
NIN = 6656
DFF = 2816
NFF = 22
LAMBDA_INIT = 0.8 - 0.6 * 1.0
KEEP_WARM = False


def isap(x):
    return hasattr(x, "tensor")


def build_program():
    nc = bass.Bass("TRN2", target_bir_lowering=False)
    ro = []

    def din(name, shape):
        ro.append(name)
        return nc.dram_tensor(name, shape, F32, kind="ExternalInput").ap()

    def dout(name, shape):
        return nc.dram_tensor(name, shape, F32, kind="ExternalOutput").ap()

    xp = din("xp", [1024, D])
    xs = din("xs", [2048, D])
    cdk = din("cdk", [512, 1024])
    cdv = din("cdv", [512, 1024])
    cwk = din("cwk", [512, 256])
    cwv = din("cwv", [512, 256])
    cvec = din("cvec", [128, 16])
    b_adaT = din("b_adaT", [128, 48])
    ngT = din("ngT", [128, 16])
    gains = din("gains", [4 * 64])
    lams = din("lams", [4 * 64])
    subln = din("subln", [128])
    sink = din("sink", [16])
    sinkC = din("sinkC", [128, 4])
    wada_p = din("wada_p", [128, 12, 4096])
    wA_p = din("wA_p", [128, 8, 3072])
    wB_p = din("wB_p", [128, 4, 3072])
    w3a_p = din("w3a_p", [128, 8, 4096])
    wout_p = din("wout_p", [128, 2, 4096])
    wgu_p = din("wgu_p", [128, 12, 4096])
    wdn_p = din("wdn_p", [128, 12, 2048])
    ident = din("ident", [128, 128])
    ropec = din("ropec", [128, 16 * 64])
    ropes = din("ropes", [128, 16 * 64])
    masks = din("masks", [128, 4 * 128])

    yp = dout("yp", [1024, D])
    ys = dout("ys", [1024, D])
    ndk = dout("ndk", [1024, 1024])
    ndv = dout("ndv", [1024, 1024])
    nwk = dout("nwk", [1024, 256])
    nwv = dout("nwv", [1024, 256])

    es = contextlib.ExitStack()
    with es:
        def sb(name, shape, dt):
            return es.enter_context(nc.sbuf_tensor(name, shape, dt))

        B = Builder(nc, ro_dram=ro)

        ARENA = 105 * 1024
        arena = sb("arena", [128, ARENA // 2], BF16)
        SCR = 40 * 1024
        scr = sb("scr", [128, SCR // 2], BF16)

        def view(t, off, shape, dt):
            n = int(np.prod(shape))
            if dt == BF16:
                v = t[:, off // 2: off // 2 + n]
            else:
                v = t[:, off // 2: off // 2 + 2 * n].bitcast(F32)
            if len(shape) == 1:
                return v
            names = " ".join(f"d{i}" for i in range(len(shape)))
            kw = {f"d{i}": s for i, s in enumerate(shape)}
            return v.rearrange(f"p ({names}) -> p {names}", **kw)

        def AR(off, shape, dt=BF16):
            return view(arena, off, shape, dt)

        def SC(off, shape, dt=F32):
            return view(scr, off, shape, dt)

        wslot = [sb(f"wslot{i}", [128, 4096], BF16) for i in range(4)]
        PS = [es.enter_context(nc.psum_tensor(f"ps{i}", [128, 512], F32)) for i in range(8)]

        def psb(i, half):
            return PS[i][:, half * 256:(half + 1) * 256].bitcast(BF16)

        ident_f = sb("ident_f", [128, 128], F32)
        ident_b = sb("ident_b", [128, 128], BF16)
        negh = sb("negh", [128, 64], F32)
        g4 = sb("g4", [128, 4, 64], F32)
        lam4 = sb("lam4", [128, 4, 64], F32)
        sublnG = sb("sublnG", [128, 128], F32)
        esink = sb("esink", [128, 16], F32)
        smallc = sb("smallc", [128, 16], F32)
        cosT = sb("cosT", [128, 16, 64], F32)
        sinT = sb("sinT", [128, 16, 2, 2, 16], F32)
        maskb = sb("maskb", [128, 4, 128], BF16)
        cv = sb("cv", [128, 8, 2], F32)
        scT = sb("scT", [128, 8, 2], BF16)
        badaT = sb("badaT", [128, 48], F32)
        ngt = sb("ngt", [128, 16], F32)
        modT = sb("modT", [128, 48, 2], F32)
        AB = sb("AB", [128, 2, 4, 8], F32)
        gbc = sb("gbc", [128, 2, D], F32)
        stats = sb("stats", [128, 512], F32)
        ones_b = sb("ones_b", [128, 128], BF16)
        GB5 = sb("GB5", [128, 5, 64], F32)
        Sel = sb("Sel", [128, 4, 128], F32)
        esC = sb("esC", [128, 4], F32)
        sublnC = sb("sublnC", [128, 1], F32)
        s_ctr = [0]
        b_ctr = [0]
        a_ctr = [0]

        def MM(out, lhsT, rhs, start=True, stop=True, skip=False):
            B.op("pe", lambda e: e.matmul(out, lhsT, rhs, start=start, stop=stop, skip_group_check=skip),
                 reads=[lhsT, rhs], writes=[out])

        def MMT(out, lhsT, rhs, start, stop, tpos):
            B.op("pe", lambda e: e.matmul(out, lhsT, rhs, start=start, stop=stop, skip_group_check=True, tile_position=tpos),
                 reads=[lhsT, rhs], writes=[out])

        def TR(out, in_, idn):
            B.op("pe", lambda e: e.transpose(out, in_, idn), reads=[in_, idn], writes=[out])

        def ACT(out, in_, func, scale=None, bias=None, accum=None):
            kw = {}
            reads = [in_]
            writes = [out]
            if scale is not None:
                kw["scale"] = scale
                if isap(scale):
                    reads.append(scale)
            if bias is not None:
                kw["bias"] = bias
                if isap(bias):
                    reads.append(bias)
            if accum is not None:
                kw["accum_out"] = accum
                writes.append(accum)
            B.op("act", lambda e: e.activation(out, in_, func, **kw), reads=reads, writes=writes, attach=(accum is None))

        def TT(eng, out, a, b, op):
            B.op(eng, lambda e: e.tensor_tensor(out, a, b, op=op), reads=[a, b], writes=[out])

        def TS(eng, out, a, s1, s2, op0, op1=None):
            reads = [a] + [s for s in (s1, s2) if isap(s)]
            if op1 is None:
                B.op(eng, lambda e: e.tensor_scalar(out, a, s1, None, op0=op0), reads=reads, writes=[out])
            else:
                B.op(eng, lambda e: e.tensor_scalar(out, a, s1, s2, op0=op0, op1=op1), reads=reads, writes=[out])

        def STT(out, a, s, b, op0, op1):
            reads = [a, b] + ([s] if isap(s) else [])
            B.op("dve", lambda e: e.scalar_tensor_tensor(out, a, s, b, op0=op0, op1=op1), reads=reads, writes=[out])

        def CP(eng, out, in_):
            if eng == "act":
                ACT(out, in_, AF.Copy)
            else:
                B.op(eng, lambda e: e.tensor_copy(out, in_), reads=[in_], writes=[out])

        def RED(out, in_):
            B.op("dve", lambda e: e.tensor_reduce(out, in_, axis=AX.X, op=ALU.add), reads=[in_], writes=[out])

        def RECIP(out, in_):
            B.op("dve", lambda e: e.reciprocal(out, in_), reads=[in_], writes=[out])

        def MEMSET(eng, ap, val):
            B.op(eng, lambda e: e.memset(ap, val), writes=[ap])

        def RSTD(out, ss, n):
            TS("pool", out, ss, 1.0 / n, EPS, ALU.mult, ALU.add)
            w = out.shape[-1] if len(out.shape) > 1 else 1
            TT("pool", out, out, negh[:, 0:w], ALU.pow)

        stat_ctr = [0]

        def stat(n):
            c = stat_ctr[0]
            if c + n > 512:
                c = 0
            stat_ctr[0] = c + n
            return stats[:, c:c + n]

        bank_ctr = [0]

        def nbank(lo=0, hi=6):
            i = bank_ctr[0]
            bank_ctr[0] = i + 1
            return lo + i % (hi - lo)

        tp_ctr = [0]

        def tpbank(banks=(6, 7)):
            i = tp_ctr[0]
            tp_ctr[0] = i + 1
            return banks[i % len(banks)]

        class WStream:
            def __init__(self):
                self.plan = []
                self.issued = 0
                self.used = 0

            def _issue(self):
                while self.issued < len(self.plan) and self.issued < self.used + 3:
                    s = wslot[self.issued % 4]
                    for src, dst in self.plan[self.issued]:
                        B.dma("pool", dst(s), src)
                    self.issued += 1

            def load(self, pieces=None):
                self._issue()
                s = wslot[self.used % 4]
                self.used += 1
                assert self.used <= self.issued
                return s
        WS = WStream()

        def wv(s, k, n):
            return s[:, 0:k * n].rearrange("p (k n) -> p k n", k=k)

        def load_cols(*a, **k):
            return WS.load()

        def slot(arr, i, n):
            return [(arr[:, i, 0:n], lambda s_: s_[:, 0:n])]

        def plan_pass():
            pl = []
            for h in range(8):
                pl.append(slot(wA_p, h, 3072))
            for g in range(4):
                pl.append(slot(wB_p, g, 3072))
            for n in range(8):
                pl.append(slot(w3a_p, n, 4096))
            for nh in range(2):
                pl.append(slot(wout_p, nh, 4096))
            for fg in range(6):
                nc_ = 512 if fg < 5 else 256
                pl.append(slot(wgu_p, 2 * fg, 8 * nc_))
                pl.append(slot(wgu_p, 2 * fg + 1, 8 * nc_))
            for nh in range(2):
                for fg in range(6):
                    nf = 4 if fg < 5 else 2
                    pl.append(slot(wdn_p, nh * 6 + fg, nf * 512))
            return pl
        WS.plan = [slot(wada_p, j, 4096) for j in range(12)] + plan_pass() + plan_pass()

        B.dma("sp", ident_f[:], ident)
        B.dma("pool", ident_b[:], ident)
        MEMSET("pool", negh[:], -0.5)
        MEMSET("dve", stats[:], 0.0)
        MEMSET("pool", ones_b[:], 1.0)
        B.dma("sp", sublnC[:], subln.rearrange("(p o) -> p o", o=1))
        TS("dve", sublnC[:], sublnC[:], 1.0 - LAMBDA_INIT, None, ALU.mult)
        B.dma("sp", cv[:].rearrange("p k t -> p (k t)"), cvec)
        B.dma("sp", badaT[:], b_adaT)
        B.dma("sp", ngt[:], ngT)
        B.dma("sp", g4[:].rearrange("p a d -> p (a d)"), gains.partition_broadcast(128))
        B.dma("sp", lam4[:].rearrange("p a d -> p (a d)"), lams.partition_broadcast(128))
        B.dma("sp", sublnG[:], subln.partition_broadcast(128))
        B.dma("sp", esink[:], sink.partition_broadcast(128))
        B.dma("sp", cosT[:].rearrange("p a d -> p (a d)"), ropec)
        B.dma("sp", sinT[:].rearrange("p a b c d -> p (a b c d)"), ropes)
        B.dma("pool", maskb[:].rearrange("p a d -> p (a d)"), masks)

        MEMSET("dve", Sel[:], 0.0)
        for c in range(4):
            MEMSET("dve", Sel[32 * c:32 * c + 1, c, :], 1.0)
        B.dma("sp", esC[:], sinkC)
        ACT(esC[:], esC[:], AF.Exp)
        CP("dve", GB5[:, 0:4, :], g4[:, 2:3, :].broadcast_to([128, 4, 64]))
        CP("dve", GB5[:, 4:5, :], g4[:, 3:4, :])
        cvf = cv[:].rearrange("p k t -> p (k t)")
        sg0 = SC(0, [16])
        ACT(sg0, cvf, AF.Sigmoid)
        TT("dve", scT[:].rearrange("p k t -> p (k t)"), sg0, cvf, ALU.mult)
        lp = SC(256, [2, 64])
        TT("dve", lp[:, 0, :], lam4[:, 0, :], lam4[:, 1, :], ALU.mult)
        TT("dve", lp[:, 1, :], lam4[:, 2, :], lam4[:, 3, :], ALU.mult)
        RED(smallc[:, 0:2], lp)
        ACT(smallc[:, 4:6], smallc[:, 0:2], AF.Exp)
        TT("dve", smallc[:, 2:3], smallc[:, 4:5], smallc[:, 5:6], ALU.subtract)
        TS("dve", smallc[:, 2:3], smallc[:, 2:3], LAMBDA_INIT, None, ALU.add)
        TS("dve", smallc[:, 3:4], smallc[:, 2:3], -1.0, None, ALU.mult)
        neglam = smallc[:, 3:4]
        ACT(esink[:], esink[:], AF.Exp)
        TS("dve", sublnG[:], sublnG[:], 1.0 - LAMBDA_INIT, None, ALU.mult)

        pm = PS[0]
        for j in range(12):
            s = load_cols(None, j * 512, 512)
            wj = wv(s, 8, 512)
            for q in range(4):
                c = j * 4 + q
                for kc in range(8):
                    MM(pm[:, c * 2:c * 2 + 2], wj[:, kc, q * 128:(q + 1) * 128], scT[:, kc, :],
                       start=(kc == 0), stop=(kc == 7))
        TT("dve", modT[:], pm[:, 0:96].rearrange("p (c t) -> p c t", t=2),
           badaT[:].unsqueeze(2).broadcast_to([128, 48, 2]), ALU.add)
        for path in range(2):
            STT(AB[:, path, 0, :], modT[:, 8:16, path], 1.0, ngt[:, 0:8], ALU.add, ALU.mult)
            CP("dve", AB[:, path, 1, :], modT[:, 0:8, path])
            STT(AB[:, path, 2, :], modT[:, 32:40, path], 1.0, ngt[:, 8:16], ALU.add, ALU.mult)
            CP("dve", AB[:, path, 3, :], modT[:, 24:32, path])

        def make_gbc(path):
            for i, c0 in enumerate((16, 40)):
                for q4 in range(2):
                    bk = tpbank()
                    for q in range(4):
                        kc = q4 * 4 + q
                        rep = SC(1024 + (kc % 2) * 512, [128])
                        CP("dve", rep, modT[:, c0 + kc, path:path + 1].broadcast_to([128, 128]))
                        TR(PS[bk][:, q * 128:(q + 1) * 128], rep, ident_f[:])
                    CP("act", gbc[:, i, q4 * 512:(q4 + 1) * 512], PS[bk][:, :])

        def norm_to_T(src_tiles, dstT, tok0s, Acol, Bcol, from_dram):
            n = len(src_tiles)
            for g0 in range(0, n, 4):
                grp = list(range(g0, min(g0 + 4, n)))
                xnbs = []
                xts = []
                rss = []
                for gi, t in enumerate(grp):
                    if from_dram:
                        xt = SC(gi * 4096, [1024])
                        B.dma("sp", xt, src_tiles[t])
                    else:
                        xt = src_tiles[t]
                    junk = SC(24576, [1024])
                    ss = stat(1)
                    ACT(junk, xt, AF.Square, accum=ss)
                    rs = stat(1)
                    RSTD(rs, ss, 1024)
                    xts.append(xt)
                    rss.append(rs)
                for gi, t in enumerate(grp):
                    xnb = SC(16384 + gi * 2048, [1024], BF16)
                    ACT(xnb, xts[gi], AF.Identity, scale=rss[gi])
                    xnbs.append(xnb)
                ng = len(grp)
                for kc in range(8):
                    bk = tpbank()
                    pt = psb(bk, kc % 2)
                    for gi in range(ng):
                        TR(pt[:, gi * 128:(gi + 1) * 128], xnbs[gi][:, kc * 128:(kc + 1) * 128], ident_b[:])
                    contiguous = all(tok0s[grp[i]] == tok0s[grp[0]] + 128 * i for i in range(ng))
                    assert contiguous
                    dst = dstT[:, kc, tok0s[grp[0]]:tok0s[grp[0]] + 128 * ng]
                    if kc % 2 == 0:
                        TS("dve", dst, pt[:, 0:128 * ng], Acol[:, kc:kc + 1], Bcol[:, kc:kc + 1], ALU.mult, ALU.add)
                    else:
                        ACT(dst, pt[:, 0:128 * ng], AF.Identity, scale=Acol[:, kc:kc + 1], bias=Bcol[:, kc:kc + 1])

        def proj_tok(hT, tok0, wsl, ncols, coff=0):
            bk = nbank()
            out = PS[bk][:, 0:ncols]
            w = wsl
            for kc in range(8):
                MM(out, hT[:, kc, tok0:tok0 + 128], w[:, kc, coff:coff + ncols], start=(kc == 0), stop=(kc == 7))
            return out

        hn_ctr = [0]

        def headnorm(ps, ga, gb_, gain_ap, rope_tile, f32_out, bf_out, f32_cols=None):
            nh = ga * gb_
            w = nh * 64
            hn_ctr[0] += 1
            alt = hn_ctr[0] % 2 == 1
            o_sq, o_t0, o_xn, o_t1, o_t2 = (6400, 7680, 8960, 10240, 11520) if alt else (0, 1280, 2560, 3840, 5120)

            def v4(x):
                return x.rearrange("p (a b d) -> p a b d", a=ga, b=gb_, d=64)

            def v3(x):
                return x.rearrange("p (h d) -> p h d", d=64)
            sq = SC(o_sq, [320])[:, 0:w]
            ACT(sq, ps, AF.Square)
            xg = SC(o_t0, [320])[:, 0:w]
            TT("dve", v4(xg), v4(ps), gain_ap, ALU.mult)
            ss = stat(nh)
            RED(ss, v3(sq))
            rs = stat(nh)
            RSTD(rs, ss, 64)
            rsb = rs.unsqueeze(2).broadcast_to([128, nh, 64])
            if rope_tile is None:
                TT("dve", v3(bf_out), v3(xg), rsb, ALU.mult)
                if f32_out is not None:
                    lo, hi = f32_cols if f32_cols is not None else (0, w)
                    TT("dve", v3(f32_out[:, lo:hi]), v3(xg[:, lo:hi]),
                       rs[:, lo // 64:hi // 64].unsqueeze(2).broadcast_to([128, (hi - lo) // 64, 64]), ALU.mult)
                return
            t1 = SC(o_t1, [320])[:, 0:w]
            t2 = SC(o_t2, [320])[:, 0:w]
            cosb = cosT[:, rope_tile, :].unsqueeze(1).broadcast_to([128, nh, 64])
            TT("dve", v3(t1), v3(xg), cosb, ALU.mult)
            xn5 = xg.rearrange("p (h a b c) -> p h a b c", a=2, b=2, c=16)
            t25 = t2.rearrange("p (h a b c) -> p h a b c", a=2, b=2, c=16)
            nsin = sinT[:, rope_tile, 0, :, :].unsqueeze(1).broadcast_to([128, nh, 2, 16])
            psin = sinT[:, rope_tile, 1, :, :].unsqueeze(1).broadcast_to([128, nh, 2, 16])
            TT("pool", t25[:, :, :, 0, :], xn5[:, :, :, 1, :], nsin, ALU.mult)
            TT("pool", t25[:, :, :, 1, :], xn5[:, :, :, 0, :], psin, ALU.mult)
            sm = SC(o_xn, [320])[:, 0:w]
            TT("dve", sm, t1, t2, ALU.add)
            TT("dve", v3(bf_out), v3(sm), rsb, ALU.mult)

        ost_ctr = [0]

        def ostage(w=512):
            i = ost_ctr[0]
            ost_ctr[0] = i + 1
            return SC(24576 + (i % 3) * 2048, [512])[:, 0:w]

        qkb_ctr = [0]

        def qkbuf(w=512):
            i = qkb_ctr[0]
            qkb_ctr[0] = i + 1
            return SC(30720 + (i % 2) * 1024, [512], BF16)[:, 0:w]

        pt_ctr = [0]

        def ptbuf():
            i = pt_ctr[0]
            pt_ctr[0] = i + 1
            return SC(32768 + (i % 4) * 1024, [512], BF16)

        OFF_HT = 0
        OFF_U = 32 * 1024
        OFF_OAT = 57 * 1024
        OFF_OBT = 73 * 1024
        OFF_X1 = 57 * 1024
        OFF_ACT = 0
        OFF_H2T = 89 * 1024
        OFF_MRG = 32 * 1024

        def run_pass(path):
            sample = (path == 1)
            xd = xs if sample else xp
            ydst = ys if sample else yp
            NTA = 16 if sample else 8
            T = NTA * 128
            hT = AR(OFF_HT, [8, T])
            oaT = AR(OFF_OAT, [8, 1024])
            obT = AR(OFF_OBT, [8, 1024])
            Acol1, Bcol1 = AB[:, path, 0, :], AB[:, path, 1, :]
            Acol2, Bcol2 = AB[:, path, 2, :], AB[:, path, 3, :]

            make_gbc(path)

            norm_to_T([xd[t * 128:(t + 1) * 128, :] for t in range(NTA)], hT,
                      [t * 128 for t in range(NTA)], Acol1, Bcol1, True)

            USZ = 12800
            NKA = 2560 if sample else 1024
            NKB = 1664 if sample else 1024
            FIN = OFF_H2T

            def FB(off, shape, dt=F32):
                return AR(FIN + off, shape, dt)

            mq = []

            def mq_tick(n=1):
                for _ in range(n):
                    if mq:
                        mq.pop(0)[1]()

            def mq_flush(upto=None):
                while mq and (upto is None or mq[0][0] <= upto):
                    mq.pop(0)[1]()

            def projA(head, uoff, banks=(6,)):
                W = wv(WS.load(), 8, 384)
                qT = AR(uoff, [1024])
                kT = AR(uoff + 2048, [NKA])
                Va = AR(uoff + 2048 + NKA * 2, [NKA // 128, 128])
                gain = g4[:, 0:2, :].unsqueeze(2).broadcast_to([128, 2, 2, 64])
                gain_k = g4[:, 1:2, :].unsqueeze(2).broadcast_to([128, 1, 2, 64])
                if sample:
                    kc_b = SC(12800, [4, 128], F32)
                    B.dma("sp", kc_b, cdk.rearrange("(t p) n -> p t n", p=128)[:, :, head * 128:(head + 1) * 128])
                    B.dma("pool", Va[:, 16:20, :], cdv.rearrange("(t p) n -> p t n", p=128)[:, :, head * 128:(head + 1) * 128])
                    pt = PS[7][:, :]
                    for t in range(4):
                        TR(pt[:, t * 128:(t + 1) * 128], kc_b[:, t, :], ident_f[:])
                    CP("act", kT[:, 2048:2560], pt[:, 0:512])
                    yield
                pend = []
                for t in range(NTA):
                    rope_tile = t if sample else None
                    pbk = PS[banks[t % len(banks)]]
                    if t < 8:
                        ps = pbk[:, 0:384]
                        for kc in range(8):
                            MM(ps, hT[:, kc, t * 128:(t + 1) * 128], W[:, kc, 0:384], start=(kc == 0), stop=(kc == 7))
                        qk = qkbuf(256)
                        f32o = ostage(256) if not sample else None
                        headnorm(ps[:, 0:256], 2, 2, gain, rope_tile, f32o, qk, f32_cols=(128, 256))
                        if not sample:
                            B.dma("sp", ndk[t * 128:(t + 1) * 128, head * 128:(head + 1) * 128], f32o[:, 128:256])
                        vsl = ps[:, 256:384]

                        def post(t=t, qk=qk):
                            pt = psb(7, t % 2)
                            for c in range(2):
                                TR(pt[:, c * 128:(c + 1) * 128], qk[:, c * 128:(c + 1) * 128], ident_b[:])
                            CP("act", qT[:, t * 128:(t + 1) * 128], pt[:, 0:128])
                            CP("dve" if sample else "act", kT[:, t * 128:(t + 1) * 128], pt[:, 128:256])
                    else:
                        ps = pbk[:, 0:256]
                        for kc in range(8):
                            MM(ps, hT[:, kc, t * 128:(t + 1) * 128], W[:, kc, 128:384], start=(kc == 0), stop=(kc == 7))
                        qk = qkbuf(128)
                        headnorm(ps[:, 0:128], 1, 2, gain_k, rope_tile, None, qk)
                        vsl = ps[:, 128:256]

                        def post(t=t, qk=qk):
                            pt = psb(7, t % 2)
                            TR(pt[:, 0:128], qk[:, 0:128], ident_b[:])
                            CP("act", kT[:, t * 128:(t + 1) * 128], pt[:, 0:128])
                    CP("act", Va[:, t, :], vsl)
                    if not sample:
                        vo = ostage(128)
                        CP("act", vo, vsl)
                        B.dma("sp", ndv[t * 128:(t + 1) * 128, head * 128:(head + 1) * 128], vo)
                    while pend:
                        pend.pop(0)()
                    pend.append(post)
                    yield
                while pend:
                    pend.pop(0)()
                yield

            def attnA(head, uoff):
                qT = AR(uoff, [1024])
                kT = AR(uoff + 2048, [NKA])
                Va = AR(uoff + 2048 + NKA * 2, [NKA // 128, 128])
                if sample:
                    jobs = [[(qb * 512, 512, [(c * 128, c) for c in range(20)])] for qb in range(2)]
                else:
                    jobs = [[(sq_ * 256, 256, [(sq_ * 256 + c * 128, sq_ * 2 + c) for c in range(2)])
                             for sq_ in (2 * jb, 2 * jb + 1)] for jb in range(2)]
                OT = [PS[2], PS[3]]
                DENC = PS[4]
                RB = PS[5]
                for job in jobs:
                    Q0 = job[0][0]
                    steps = []
                    for (q0, nq, chunks) in job:
                        for ci, (koff, vch) in enumerate(chunks):
                            steps.append((q0, nq, koff, vch, ci == 0, ci == len(chunks) - 1))

                    def qk_exp(st):
                        q0, nq, koff, vch, first, last = st
                        pts = []
                        for sub in range(2):
                            S = PS[sub][:, 0:nq]
                            MM(S, kT[sub * 64:(sub + 1) * 64, koff:koff + 128], qT[sub * 64:(sub + 1) * 64, q0:q0 + nq])
                            P = ptbuf()[:, 0:nq]
                            ACT(P, S, AF.Exp, scale=0.125)
                            pts.append(P)
                        return pts

                    def pv_den(st, pts):
                        q0, nq, koff, vch, first, last = st
                        c0 = q0 - Q0
                        for sub in range(2):
                            MM(OT[sub][:, c0:c0 + nq], Va[:, vch, :], pts[sub], start=first, stop=last)
                        for sub in range(2):
                            for j in range(nq // 128):
                                c = c0 // 128 + j
                                MMT(DENC[32 * c:32 * c + 32, sub * 128:(sub + 1) * 128], ones_b[:, 0:32],
                                    pts[sub][:, j * 128:(j + 1) * 128], first and sub == 0, last, (0, 32 * c))

                    prev = None
                    for i, st in enumerate(steps):
                        pts = qk_exp(st)
                        if prev is not None:
                            pv_den(*prev)
                        prev = (st, pts)
                        mq_tick(1)
                        yield
                    pv_den(*prev)
                    tag = a_ctr[0]
                    a_ctr[0] += 1
                    mq_flush()
                    fbase = FIN

                    def FB(off, shape, dt=F32, fbase=fbase):
                        return AR(fbase + off, shape, dt)
                    o0 = FB(0, [512])
                    o1s = FB(2048, [512])
                    tt_ = FB(8192, [512])
                    uu = FB(10240, [512])
                    oa = FB(12288, [512])
                    sqb = FB(14336, [512], BF16)
                    dcc = FB(4096, [256])
                    CP("dve", dcc, DENC[:, 0:256])
                    CP("dve" if sample else "act", o0, OT[0][:, :])
                    CP("dve" if sample else "act", o1s, OT[1][:, :])
                    tasks = []
                    tasks.append(lambda: RECIP(dcc, dcc))

                    def bcast(sub):
                        for c in range(4):
                            MM(RB[:, c * 128:(c + 1) * 128], Sel[:, c, :], dcc[:, sub * 128:(sub + 1) * 128], start=True, stop=True)
                    tasks.append(lambda: bcast(0))
                    tasks.append(lambda: TT("dve", tt_, o0, RB[:, :], ALU.mult))
                    tasks.append(lambda: bcast(1))
                    tasks.append(lambda: TT("dve", uu, o1s, RB[:, :], ALU.mult))
                    tasks.append(lambda: STT(oa, uu, neglam, tt_, ALU.mult, ALU.add))
                    tasks.append(lambda: TT("dve", sqb, oa, oa, ALU.mult))
                    holder = []

                    def part2a(sqb=sqb, holder=holder, FB=FB):
                        for qt in range(4):
                            MM(PS[7][:, qt:qt + 1], sqb[:, qt * 128:(qt + 1) * 128], ones_b[:, 0:1], start=True, stop=True)
                        rsq = stat(4)
                        TS("dve", rsq, PS[7][:, 0:4], 1.0 / 128, EPS, ALU.mult, ALU.add)
                        TT("pool", rsq, rsq, negh[:, 0:4], ALU.pow)
                        for qt in range(4):
                            rep = FB(qt * 512, [128])
                            CP("dve", rep, rsq[:, qt:qt + 1].broadcast_to([128, 128]))
                            holder.append(rep)

                    def part2b(holder=holder, Q0=Q0, oa=oa):
                        for qt in range(4):
                            TR(PS[7][:, qt * 128:(qt + 1) * 128], holder[qt], ident_f[:])
                        STT(oaT[:, head, Q0:Q0 + 512], oa, sublnC[:, 0:1], PS[7][:, :], ALU.mult, ALU.mult)
                    tasks.append(lambda: None)
                    tasks.append(part2a)
                    tasks.append(lambda: None)
                    tasks.append(lambda: None)
                    tasks.append(part2b)
                    for tk in tasks:
                        mq.append((tag, tk))
                    yield
                yield

            def projB(g4i, uoff):
                W = wv(WS.load(), 8, 384)
                qT = AR(uoff, [2, 1024])
                kT = AR(uoff + 4096, [NKB])
                Vb = AR(uoff + 4096 + NKB * 2, [NKB // 128, 2, 64])
                gk = g4[:, 3:4, :].unsqueeze(2).broadcast_to([128, 1, 1, 64])
                g5 = GB5[:].unsqueeze(1)
                NTB = 9 if sample else 8
                if sample:
                    kc_b = SC(12800, [4, 2, 64], F32)
                    src = cwk.rearrange("(t p) n -> p t n", p=128)[:, :, g4i * 64:(g4i + 1) * 64]
                    srcv = cwv.rearrange("(t p) n -> p t n", p=128)[:, :, g4i * 64:(g4i + 1) * 64]
                    for dup in range(2):
                        B.dma("sp", kc_b[:, :, dup, :], src)
                        B.dma("pool", Vb[:, 9:13, dup, :], srcv)
                    pt = PS[7][:, :]
                    for t in range(4):
                        TR(pt[:, t * 128:(t + 1) * 128], kc_b[:, t, :, :].rearrange("p a d -> p (a d)"), ident_f[:])
                    CP("act", kT[:, 1152:1664], pt[:, 0:512])
                    yield
                pend = []
                for t in range(NTB):
                    rope_tile = t if sample else None
                    posts = []
                    if t < 8:
                        ps = PS[6][:, 0:384]
                        for kc in range(8):
                            MM(ps, hT[:, kc, t * 128:(t + 1) * 128], W[:, kc, 0:384], start=(kc == 0), stop=(kc == 7))
                        qb_ = qkbuf(320)
                        f32o = ostage(320) if not sample else None
                        headnorm(ps[:, 0:320], 1, 5, g5, rope_tile, f32o, qb_, f32_cols=(256, 320))
                        kb_ = qb_[:, 256:320]
                        if not sample:
                            B.dma("sp", nwk[t * 128:(t + 1) * 128, g4i * 64:(g4i + 1) * 64], f32o[:, 256:320])
                        vsl = ps[:, 320:384]

                        def post1(t=t, qb_=qb_):
                            pt = psb(7, t % 2)
                            for c in range(2):
                                TR(pt[:, c * 128:(c + 1) * 128], qb_[:, c * 128:(c + 1) * 128], ident_b[:])
                            CP("act", qT[:, :, t * 128:(t + 1) * 128], pt[:, 0:256].rearrange("p (c k) -> p c k", c=2))
                        posts.append(post1)
                    else:
                        ps = PS[6][:, 0:128]
                        for kc in range(8):
                            MM(ps, hT[:, kc, t * 128:(t + 1) * 128], W[:, kc, 256:384], start=(kc == 0), stop=(kc == 7))
                        kb_ = SC(15872, [64], BF16)
                        headnorm(ps[:, 0:64], 1, 1, gk, rope_tile, None, kb_)
                        vsl = ps[:, 64:128]
                    CP("act", Vb[:, t, 0, :], vsl)
                    CP("dve" if sample else "act", Vb[:, t, 1, :], vsl)
                    if not sample:
                        vo = ostage(64)
                        CP("act", vo, vsl)
                        B.dma("sp", nwv[t * 128:(t + 1) * 128, g4i * 64:(g4i + 1) * 64], vo)

                    kd = SC(15360 + (t % 2) * 256, [2, 64], BF16)
                    CP("dve", kd[:, 0, :], kb_)
                    CP("dve", kd[:, 1, :], kb_)

                    def post2(t=t, kd=kd):
                        pt = psb(7, t % 2)
                        TR(pt[:, 256:384], kd.rearrange("p a d -> p (a d)"), ident_b[:])
                        CP("dve", kT[:, t * 128:(t + 1) * 128], pt[:, 256:384])
                    posts.append(post2)
                    while pend:
                        pend.pop(0)()
                    pend.extend(posts)
                    yield
                while pend:
                    pend.pop(0)()
                yield

            def attnB(g4i, uoff):
                qT = AR(uoff, [2, 1024])
                kT = AR(uoff + 4096, [NKB])
                Vb = AR(uoff + 4096 + NKB * 2, [NKB // 128, 2, 64])
                for qt in range(8):
                    if sample:
                        chunks = []
                        if qt == 0:
                            chunks.append((8 * 128, 8, 2))
                        else:
                            chunks.append(((qt - 1) * 128, qt - 1, 0))
                        chunks.append((qt * 128, qt, None))
                        if qt == 7:
                            chunks.append((8 * 128, 8, 3))
                        else:
                            chunks.append(((qt + 1) * 128, qt + 1, 1))
                        chunks += [(1152 + c * 128, 9 + c, None) for c in range(4)]
                    else:
                        s0 = (qt // 2) * 2
                        chunks = [((s0 + c) * 128, s0 + c, None) for c in range(2)]
                    jb = b_ctr[0]
                    b_ctr[0] += 1
                    mq_flush(upto=1000000 + jb - 2)
                    OTb = PS[2 + (jb % 2)]
                    DENb = PS[4 + (jb % 2)]

                    def qk_exp(ch):
                        koff, vch, mk = ch
                        P = ptbuf()
                        for par in range(2):
                            S = PS[par][:, 0:256]
                            MM(S.rearrange("p (a q) -> p a q", a=2), kT[par * 64:(par + 1) * 64, koff:koff + 128],
                               qT[par * 64:(par + 1) * 64, :, qt * 128:(qt + 1) * 128])
                            ACT(P[:, par * 256:(par + 1) * 256], S, AF.Exp, scale=0.125)
                        if mk is not None:
                            P4 = P.rearrange("p (a q) -> p a q", a=4)
                            TT("dve", P4, P4, maskb[:, mk:mk + 1, :].broadcast_to([128, 4, 128]), ALU.mult)
                        return P

                    def pv_den(ci, ch, P):
                        first = (ci == 0)
                        last = (ci == len(chunks) - 1)
                        MM(OTb[:, :], Vb[:, ch[1], :, :].rearrange("p a d -> p (a d)"), P, start=first, stop=last)
                        for c in range(4):
                            MMT(DENb[32 * c:32 * c + 32, 0:128], ones_b[:, 0:32], P[:, c * 128:(c + 1) * 128], first, last, (0, 32 * c))

                    prev = None
                    for ci, ch in enumerate(chunks):
                        P = qk_exp(ch)
                        if prev is not None:
                            pv_den(*prev)
                        prev = (ci, ch, P)
                        mq_tick(1)
                        yield
                    pv_den(*prev)
                    dn = SC(16384 + (jb % 2) * 2048, [512])
                    dnc = SC(20480 + (jb % 2) * 512, [128])
                    TS("dve", dnc, DENb[:, 0:128], esC[:, g4i:g4i + 1], None, ALU.add)
                    c0 = g4i * 2
                    tagb = 1000000 + jb
                    mq.append((tagb, lambda dnc=dnc: RECIP(dnc, dnc)))

                    def bcastb(dnc=dnc, DENb=DENb, dn=dn):
                        for c in range(4):
                            MM(DENb[:, c * 128:(c + 1) * 128], Sel[:, c, :], dnc, start=True, stop=True)
                        CP("dve", dn, DENb[:, :])
                    mq.append((tagb, bcastb))

                    def fmul(par, dn=dn, OTb=OTb, c0=c0, qt=qt):
                        ps_ = slice(par * 64, (par + 1) * 64)
                        cs_ = slice(par * 256, (par + 1) * 256)
                        TT("dve", obT[ps_, c0:c0 + 2, qt * 128:(qt + 1) * 128],
                           OTb[ps_, cs_].rearrange("p (a q) -> p a q", a=2),
                           dn[ps_, cs_].rearrange("p (a q) -> p a q", a=2), ALU.mult)
                    mq.append((tagb, lambda fmul=fmul: fmul(0)))
                    mq.append((tagb, lambda fmul=fmul: fmul(1)))
                    yield

            groups = [("A", h) for h in range(8)] + [("B", g) for g in range(4)]
            gens = []
            for gi, (kind, idx) in enumerate(groups):
                uoff = OFF_U + (gi % 2) * USZ
                if kind == "A":
                    gens.append((projA(idx, uoff, banks=((6, 0, 1, 2, 3) if gi == 0 else (6,))), attnA(idx, uoff), (2 if sample else 1)))
                else:
                    gens.append((projB(idx, uoff), attnB(idx, uoff), (5 if sample else 2)))
            for _ in gens[0][0]:
                pass
            for k in range(len(gens)):
                a = gens[k][1]
                ratio = gens[k][2]
                p = gens[k + 1][0] if k + 1 < len(gens) else None
                cnt = 0
                for _ in a:
                    cnt += 1
                    if p is not None and cnt % ratio == 0:
                        try:
                            next(p)
                        except StopIteration:
                            p = None
                if p is not None:
                    for _ in p:
                        pass
            mq_flush()

            mrg = AR(OFF_MRG, [8, 1024])
            for n in range(8):
                ws_ = WS.load()
                W = wv(ws_, 8, 512)
                for tb in range(2):
                    tk = slice(tb * 512, (tb + 1) * 512)
                    pb = [PS[nbank(0, 8)] for _ in range(4)]
                    srcs = [oaT, obT, hT, hT]
                    for i in range(4):
                        for kc in range(8):
                            MM(pb[i][:, :], W[:, kc, i * 128:(i + 1) * 128], srcs[i][:, kc, tk], start=(kc == 0), stop=(kc == 7))
                    sb0 = 0 if (n * 2 + tb) % 2 == 0 else 16384
                    sga = SC(sb0, [512])
                    sgb = SC(sb0 + 2048, [512])
                    ACT(sga, pb[2][:, :], AF.Sigmoid)
                    ACT(sgb, pb[3][:, :], AF.Sigmoid)
                    m1 = SC(sb0 + 4096, [512])
                    m2 = SC(sb0 + 6144, [512])
                    TT("dve", m1, sga, pb[0][:, :], ALU.mult)
                    TT("dve", m2, sgb, pb[1][:, :], ALU.mult)
                    TT("pool", mrg[:, n, tk], m1, m2, ALU.add)

            x1 = AR(OFF_X1, [8, 1024], F32)
            for nh in range(2):
                W = wv(load_cols(None, nh * 512, 512), 8, 512)
                for t in range(8):
                    bk = nbank(0, 8)
                    o = PS[bk][:, :]
                    for kc in range(8):
                        MM(o, mrg[:, kc, t * 128:(t + 1) * 128], W[:, kc, :], start=(kc == 0), stop=(kc == 7))
                    xr = SC(8192 + (t % 3) * 2048, [512])
                    B.dma("sp", xr, xd[t * 128:(t + 1) * 128, nh * 512:(nh + 1) * 512])
                    tmp = SC((t % 2) * 2048, [512])
                    TT("dve", tmp, o, gbc[:, 0, nh * 512:(nh + 1) * 512], ALU.mult)
                    TT("pool" if t % 2 else "dve", x1[:, t, nh * 512:(nh + 1) * 512], tmp, xr, ALU.add)

            h2T = AR(OFF_H2T, [8, 1024])
            norm_to_T([x1[:, t, :] for t in range(8)], h2T, [t * 128 for t in range(8)], Acol2, Bcol2, False)
            actT = AR(OFF_ACT, [NFF, 1024])
            for fg in range(6):
                nc_ = 512 if fg < 5 else 256
                Wg = wv(load_cols(None, fg * 512, nc_), 8, nc_)
                Wu = wv(load_cols(None, fg * 512, nc_), 8, nc_)
                for j in range(nc_ // 128):
                    f = fg * 4 + j
                    for tb in range(2):
                        tk = slice(tb * 512, (tb + 1) * 512)
                        pg = PS[nbank(0, 8)]
                        pu = PS[nbank(0, 8)]
                        for kc in range(8):
                            MM(pg[:, :], Wg[:, kc, j * 128:(j + 1) * 128], h2T[:, kc, tk], start=(kc == 0), stop=(kc == 7))
                        for kc in range(8):
                            MM(pu[:, :], Wu[:, kc, j * 128:(j + 1) * 128], h2T[:, kc, tk], start=(kc == 0), stop=(kc == 7))
                        sg = SC(((f * 2 + tb) % 2) * 2048, [512])
                        ACT(sg, pg[:, :], AF.Silu)
                        TT("dve", actT[:, f, tk], sg, pu[:, :], ALU.mult)
            for nh in range(2):
                for fg in range(6):
                    nf = 4 if fg < 5 else 2
                    Wd = wv(WS.load(), nf, 512)
                    for j in range(nf):
                        f = fg * 4 + j
                        for t in range(8):
                            MM(PS[t][:, :], actT[:, f, t * 128:(t + 1) * 128], Wd[:, j, :], start=(f == 0), stop=(f == NFF - 1))
                for t in range(8):
                    tmp = SC(4096 + (t % 2) * 2048, [512])
                    TT("dve", tmp, PS[t][:, :], gbc[:, 1, nh * 512:(nh + 1) * 512], ALU.mult)
                    yo = ostage()
                    TT("pool" if t % 2 else "dve", yo, tmp, x1[:, t, nh * 512:(nh + 1) * 512], ALU.add)
                    B.dma("sp", ydst[t * 128:(t + 1) * 128, nh * 512:(nh + 1) * 512], yo)

        run_pass(0)
        run_pass(1)
        B.emit()
        print("ops:", B.nops, {e: len(s) for e, s in B.streams.items()})
    return nc


_PROGRAM = None


def _rope_tables(pos):
    quarter = 16
    freqs = (10000.0 ** (-np.arange(quarter, dtype=np.float32) / quarter)).astype(np.float32)
    row = (pos // 64).astype(np.float32)
    col = (pos % 64).astype(np.float32)
    ang_r = row[:, None] * freqs[None, :]
    ang_c = col[:, None] * freqs[None, :]
    cr, sr = np.cos(ang_r).astype(np.float32), np.sin(ang_r).astype(np.float32)
    cc, sc = np.cos(ang_c).astype(np.float32), np.sin(ang_c).astype(np.float32)
    cos = np.concatenate([cr, cr, cc, cc], axis=1)
    sin = np.stack([sr, sc], axis=1)
    sins = np.stack([-sin, sin], axis=1)
    return cos, sins.reshape(len(pos), 64)


def _to_tiles(a, nt):
    w = a.shape[1]
    return np.ascontiguousarray(a.reshape(nt, 128, w).transpose(1, 0, 2).reshape(128, nt * w))


def kernel(x_prompt, x_sample, cache_diff_k, cache_diff_v, cache_win_k, cache_win_v, c, c_ctx,
           w_ada, b_ada, norm1_g, w_in, qn_a, kn_a, lambda_q1, lambda_k1, lambda_q2, lambda_k2,
           subln_g, qn_b, kn_b, sink, w_oa, w_ob, w_out, norm2_g, w_gate, w_up, w_down):
    global _PROGRAM
    f = lambda a: np.ascontiguousarray(np.asarray(a, dtype=np.float32))
    x_prompt, x_sample = f(x_prompt), f(x_sample)
    if _PROGRAM is None:
        _PROGRAM = build_program()
    nc = _PROGRAM

    def featT(v, k):
        return np.ascontiguousarray(f(v).reshape(k, 128).T)

    def pack_cols(W, col_lists, width):
        out = np.zeros((128, len(col_lists), width), np.float32)
        for i, cols_ in enumerate(col_lists):
            Wc = W[:, cols_]
            n = Wc.shape[1]
            out[:, i, :8 * n] = Wc.reshape(8, 128, n).transpose(1, 0, 2).reshape(128, 8 * n)
        return out

    ar = np.arange
    Win = f(w_in[0])
    colsA = [np.concatenate([ar(h * 128, (h + 1) * 128), 1024 + ar(h * 128, (h + 1) * 128),
                             2048 + ar(h * 128, (h + 1) * 128)]) for h in range(8)]
    colsB = [np.concatenate([3072 + ar(g * 256, (g + 1) * 256), 4096 + ar(g * 64, (g + 1) * 64),
                             4352 + ar(g * 64, (g + 1) * 64)]) for g in range(4)]
    W3 = np.concatenate([f(w_oa[0]), f(w_ob[0]), Win[:, 4608:5632], Win[:, 5632:6656]], axis=1)
    cols3 = [np.concatenate([ar(n * 128, (n + 1) * 128), 1024 + ar(n * 128, (n + 1) * 128),
                             2048 + ar(n * 128, (n + 1) * 128), 3072 + ar(n * 128, (n + 1) * 128)]) for n in range(8)]
    Wg, Wu = f(w_gate[0]), f(w_up[0])
    wgu = np.zeros((128, 12, 4096), np.float32)
    for fg in range(6):
        nc_ = 512 if fg < 5 else 256
        wgu[:, 2 * fg:2 * fg + 1, :] = pack_cols(Wg, [ar(fg * 512, fg * 512 + nc_)], 4096)
        wgu[:, 2 * fg + 1:2 * fg + 2, :] = pack_cols(Wu, [ar(fg * 512, fg * 512 + nc_)], 4096)
    Wd = f(w_down[0])
    wdn = np.zeros((128, 12, 2048), np.float32)
    for nh in range(2):
        for fg in range(6):
            nf = 4 if fg < 5 else 2
            blk = Wd[fg * 512:fg * 512 + nf * 128, nh * 512:(nh + 1) * 512]
            wdn[:, nh * 6 + fg, :nf * 512] = blk.reshape(nf, 128, 512).transpose(1, 0, 2).reshape(128, nf * 512)
    shared = {
        "b_adaT": featT(b_ada[0], 48),
        "ngT": np.ascontiguousarray(np.concatenate([featT(norm1_g[0], 8), featT(norm2_g[0], 8)], axis=1)),
        "gains": np.concatenate([f(qn_a[0]), f(kn_a[0]), f(qn_b[0]), f(kn_b[0])]),
        "lams": np.concatenate([f(lambda_q1[0]), f(lambda_k1[0]), f(lambda_q2[0]), f(lambda_k2[0])]),
        "subln": f(subln_g[0]), "sink": f(sink[0]),
        "sinkC": np.ascontiguousarray(np.stack([np.repeat(np.stack([f(sink[0])[g * 4 + ((c % 2) * 2 + c // 2)] for c in range(4)]), 32)
                                                for g in range(4)], axis=1)),
        "wada_p": pack_cols(f(w_ada[0]), [ar(j * 512, (j + 1) * 512) for j in range(12)], 4096),
        "wA_p": pack_cols(Win, colsA, 3072), "wB_p": pack_cols(Win, colsB, 3072),
        "w3a_p": pack_cols(W3, cols3, 4096),
        "wout_p": pack_cols(f(w_out[0]), [ar(nh * 512, (nh + 1) * 512) for nh in range(2)], 4096),
        "wgu_p": wgu, "wdn_p": wdn,
        "ident": np.eye(128, dtype=np.float32),
    }
    j = np.arange(128)[:, None]
    q = np.arange(128)[None, :]
    triL = (j >= q).astype(np.float32)
    triR = (j <= q).astype(np.float32)
    in_maps = []
    for core in range(8):
        b, hf = core // 2, core % 2
        own = np.arange(hf * 1024, (hf + 1) * 1024)
        if hf == 0:
            oth = np.arange(1024, 2048)
        else:
            oth = np.concatenate([np.arange(896, 1024), np.arange(0, 896)])
        order = np.concatenate([own, oth])
        cos, sins = _rope_tables(order)
        m = np.stack([triL, triR, triL * float(hf), triR * float(1 - hf)], axis=1).reshape(128, 512)
        cvec = np.stack([featT(c_ctx, 8), featT(c[b], 8)], axis=2).reshape(128, 16)
        d = dict(shared)
        d.update({
            "xp": x_prompt[core * 4:(core + 1) * 4].reshape(1024, 1024),
            "xs": np.ascontiguousarray(x_sample[b][order]),
            "cdk": f(cache_diff_k[b, 0]).reshape(512, 1024),
            "cdv": f(cache_diff_v[b, 0]).reshape(512, 1024),
            "cwk": f(cache_win_k[b, 0]).reshape(512, 256),
            "cwv": f(cache_win_v[b, 0]).reshape(512, 256),
            "cvec": np.ascontiguousarray(cvec),
            "ropec": _to_tiles(cos, 16), "ropes": _to_tiles(sins, 16),
            "masks": np.ascontiguousarray(m),
        })
        in_maps.append(d)
    res = run_bass_kernel_spmd(nc, in_maps, core_ids=list(range(8)))
    R = res.results
    y_p = np.concatenate([r["yp"].reshape(4, 256, 1024) for r in R], axis=0)
    y_s = np.stack([np.concatenate([R[2 * b]["ys"], R[2 * b + 1]["ys"]], axis=0) for b in range(4)], axis=0)
    n_dk = np.concatenate([r["ndk"].reshape(4, 1, 256, 8, 2, 64) for r in R], axis=0)
    n_dv = np.concatenate([r["ndv"].reshape(4, 1, 256, 8, 128) for r in R], axis=0)
    n_wk = np.concatenate([r["nwk"].reshape(4, 1, 256, 4, 64) for r in R], axis=0)
    n_wv = np.concatenate([r["nwv"].reshape(4, 1, 256, 4, 64) for r in R], axis=0)
    return (y_p.astype(np.float32), y_s.astype(np.float32), n_dk.astype(np.float32),
            n_dv.astype(np.float32), n_wk.astype(np.float32), n_wv.astype(np.float32))
```
